# Optimizing a Trainium2 kernel written in Bass

```python
import math
import jax, jax.numpy as jnp
from jax import lax
import numpy as np

D_MODEL = 1024
BATCH = 32
SEQ = 256
DEPTH = 2
DEC_BATCH = 2
DEC_SEQ = 1024
PAST_LEN = 256

GRID_W = 64
ROPE_THETA = 10000.0
EPS = 1e-6
NEG_INF = -1e30
Q_BLOCK = 128
CONV_CH = D_MODEL // 2
CONV_WIDTH = 31
DIFF_HEADS = 4
DIFF_SUB_DIM = 64
DIFF_V_DIM = 2 * DIFF_SUB_DIM
DIFF_WIDTH = DIFF_HEADS * DIFF_V_DIM
HEAD_DIM = 64
GQA_HEADS = D_MODEL // HEAD_DIM
GQA_KV_HEADS = 4
GQA_GROUP = GQA_HEADS // GQA_KV_HEADS
WINDOW = 128
FFN_HIDDEN = -(-8 * D_MODEL // (3 * 256)) * 256
N_MOD = 6

kernel_name = 'hybrid_diffusion_prefix_step'


def rms_norm(x, g):
    xf = x.astype(jnp.float32)
    y = xf * lax.rsqrt(jnp.mean(xf * xf, axis=-1, keepdims=True) + EPS)
    return (y * g.astype(jnp.float32)).astype(x.dtype)


def layer_norm(x, g, b):
    xf = x.astype(jnp.float32)
    mu = jnp.mean(xf, axis=-1, keepdims=True)
    xc = xf - mu
    y = xc * lax.rsqrt(jnp.mean(xc * xc, axis=-1, keepdims=True) + EPS)
    return (y * g.astype(jnp.float32) + b.astype(jnp.float32)).astype(x.dtype)


def rope_1d(x, pos):
    d = x.shape[-1]
    inv = 1.0 / (ROPE_THETA ** (jnp.arange(0, d, 2, dtype=jnp.float32) / d))
    ang = pos.astype(jnp.float32)[:, None] * inv[None, :]
    cos = jnp.cos(ang)[None, :, None, :]
    sin = jnp.sin(ang)[None, :, None, :]
    xf = x.astype(jnp.float32)
    x1, x2 = xf[..., : d // 2], xf[..., d // 2:]
    return jnp.concatenate([x1 * cos - x2 * sin, x2 * cos + x1 * sin], axis=-1).astype(x.dtype)


def axial_rope(x):
    shp = x.shape
    B, S, D = shp[0], shp[1], shp[-1]
    xr = x.reshape(B, S, -1, D)
    t = jnp.arange(S)
    rows, cols = t // GRID_W, t % GRID_W
    half = D // 2
    out = jnp.concatenate([rope_1d(xr[..., :half], rows), rope_1d(xr[..., half:], cols)], axis=-1)
    return out.reshape(shp)


def blocked_queries(fn, q):
    B, S = q.shape[:2]
    nb = S // Q_BLOCK
    qb = jnp.moveaxis(q.reshape((B, nb, Q_BLOCK) + q.shape[2:]), 1, 0)
    o = lax.map(fn, qb)
    return jnp.moveaxis(o, 0, 1).reshape((B, S) + o.shape[3:])


def conformer_conv(u, conv_w, conv_b, ln_g, ln_b):
    a, gt = jnp.split(u, 2, axis=-1)
    z = a * jax.nn.sigmoid(gt)
    z = lax.conv_general_dilated(
        z, conv_w.astype(z.dtype), window_strides=(1,),
        padding=[(CONV_WIDTH // 2, CONV_WIDTH // 2)],
        dimension_numbers=('NWC', 'WIO', 'NWC'),
        feature_group_count=CONV_CH) + conv_b
    return jax.nn.silu(layer_norm(z, ln_g, ln_b))


def diff_attention(q, k, v, lam):
    scale = DIFF_SUB_DIM ** -0.5
    def one(qb):
        s = jnp.einsum('bqhjd,bkhjd->bhjqk', qb, k).astype(jnp.float32) * scale
        p = jax.nn.softmax(s, axis=-1)
        w = p[:, :, 0] - lam * p[:, :, 1]
        return jnp.einsum('bhqk,bkhe->bqhe', w.astype(v.dtype), v)
    return blocked_queries(one, q)


def sink_attention(qb, k, v, sink, mask):
    s = jnp.einsum('bqhgd,bkhd->bhgqk', qb, k).astype(jnp.float32) * (HEAD_DIM ** -0.5)
    if mask is not None:
        s = jnp.where(mask, s, NEG_INF)
    sk = jnp.broadcast_to(sink.astype(jnp.float32)[None, :, :, None, None], s.shape[:-1] + (1,))
    p = jax.nn.softmax(jnp.concatenate([s, sk], axis=-1), axis=-1)[..., :-1]
    return jnp.einsum('bhgqk,bkhd->bqhgd', p.astype(v.dtype), v)


def windowed_sink_attention(q, k, v, k_ctx, v_ctx, sink):
    B, S = q.shape[:2]
    nb = S // Q_BLOCK
    span = Q_BLOCK + 2 * WINDOW
    pad = ((0, 0), (WINDOW, WINDOW), (0, 0), (0, 0))
    kp, vp = jnp.pad(k, pad), jnp.pad(v, pad)
    i = jnp.arange(Q_BLOCK)[:, None]
    j = jnp.arange(span)[None, :]
    band = jnp.abs(i - j + WINDOW) <= WINDOW
    ctx_valid = jnp.ones((Q_BLOCK, k_ctx.shape[1]), dtype=bool)
    def one(b):
        start = b * Q_BLOCK
        qb = lax.dynamic_slice_in_dim(q, start, Q_BLOCK, axis=1)
        kw = lax.dynamic_slice_in_dim(kp, start, span, axis=1)
        vw = lax.dynamic_slice_in_dim(vp, start, span, axis=1)
        kpos = start - WINDOW + j
        valid = band & (kpos >= 0) & (kpos < S)
        keys = jnp.concatenate([kw, k_ctx], axis=1)
        vals = jnp.concatenate([vw, v_ctx], axis=1)
        mask = jnp.concatenate([valid, ctx_valid], axis=1)
        return sink_attention(qb, keys, vals, sink, mask)
    o = lax.map(one, jnp.arange(nb))
    return jnp.moveaxis(o, 0, 1).reshape(q.shape)


def mixer_conv_diff(h, p, layer, ctx_kv):
    w_in, conv_w, conv_b, ln_g, ln_b, lam_vec, subln_g, w_out = p
    B, S, _ = h.shape
    proj = h @ w_in
    o0 = 2 * CONV_CH
    u = proj[..., :o0]
    q = proj[..., o0:o0 + DIFF_WIDTH].reshape(B, S, DIFF_HEADS, 2, DIFF_SUB_DIM)
    k = proj[..., o0 + DIFF_WIDTH:o0 + 2 * DIFF_WIDTH].reshape(B, S, DIFF_HEADS, 2, DIFF_SUB_DIM)
    v = proj[..., o0 + 2 * DIFF_WIDTH:].reshape(B, S, DIFF_HEADS, DIFF_V_DIM)
    conv_out = conformer_conv(u, conv_w, conv_b, ln_g, ln_b)
    if ctx_kv is None:
        k_all, v_all, new_kv = k, v, (k, v)
    else:
        q, k = axial_rope(q), axial_rope(k)
        k_all = jnp.concatenate([k, ctx_kv[0]], axis=1)
        v_all = jnp.concatenate([v, ctx_kv[1]], axis=1)
        new_kv = None
    lam_init = 0.8 - 0.6 * math.exp(-0.3 * layer)
    lf = lam_vec.astype(jnp.float32)
    lam = jnp.exp(jnp.sum(lf[0] * lf[1])) - jnp.exp(jnp.sum(lf[2] * lf[3])) + lam_init
    o = diff_attention(q, k_all, v_all, lam)
    o = rms_norm(o, subln_g) * (1.0 - lam_init)
    mixed = jnp.concatenate([conv_out, o.reshape(B, S, DIFF_WIDTH)], axis=-1) @ w_out
    return mixed, new_kv


def mixer_window_gqa(h, p, ctx_kv):
    w_qkv, sink, w_out = p
    B, S, _ = h.shape
    nq, nkv = GQA_HEADS * HEAD_DIM, GQA_KV_HEADS * HEAD_DIM
    proj = h @ w_qkv
    q = proj[..., :nq].reshape(B, S, GQA_KV_HEADS, GQA_GROUP, HEAD_DIM)
    k = proj[..., nq:nq + nkv].reshape(B, S, GQA_KV_HEADS, HEAD_DIM)
    v = proj[..., nq + nkv:].reshape(B, S, GQA_KV_HEADS, HEAD_DIM)
    sink_g = sink.reshape(GQA_KV_HEADS, GQA_GROUP)
    if ctx_kv is None:
        o = blocked_queries(lambda qb: sink_attention(qb, k, v, sink_g, None), q)
        new_kv = (k, v)
    else:
        q, k = axial_rope(q), axial_rope(k)
        o = windowed_sink_attention(q, k, v, ctx_kv[0], ctx_kv[1], sink_g)
        new_kv = None
    return o.reshape(B, S, nq) @ w_out, new_kv


def trunk_layer(x, cond, layer, common, mix, ctx_kv):
    mod_w, mod_b, norm_g, w_gu, w_down = common
    m = jax.nn.silu(cond) @ mod_w + mod_b
    sh1, sc1, g1, sh2, sc2, g2 = jnp.split(m[:, None, :], N_MOD, axis=-1)
    h = rms_norm(x, norm_g[0]) * (1 + sc1) + sh1
    if layer % 2 == 0:
        mixed, kv = mixer_conv_diff(h, mix, layer, ctx_kv)
    else:
        mixed, kv = mixer_window_gqa(h, mix, ctx_kv)
    x = x + g1 * rms_norm(mixed, norm_g[1])
    h = rms_norm(x, norm_g[2]) * (1 + sc2) + sh2
    gate, up = jnp.split(h @ w_gu, 2, axis=-1)
    f = (jax.nn.silu(gate) * up) @ w_down
    x = x + g2 * rms_norm(f, norm_g[3])
    return x, kv


def setup_inputs(seed: int = 0) -> dict:
    key = jax.random.key(seed)
    ks = iter(jax.random.split(key, 40))
    def nrm(shape, scale):
        return jax.random.normal(next(ks), shape, jnp.float32) * scale
    D = D_MODEL
    s = D ** -0.5
    return {
        'x_prompt': nrm((BATCH, SEQ, D), 1.0),
        'x_sample': nrm((DEC_BATCH, DEC_SEQ, D), 1.0),
        'cache_k0': nrm((DEC_BATCH, PAST_LEN, DIFF_HEADS, 2, DIFF_SUB_DIM), 1.0),
        'cache_v0': nrm((DEC_BATCH, PAST_LEN, DIFF_HEADS, DIFF_V_DIM), 1.0),
        'cache_k1': nrm((DEC_BATCH, PAST_LEN, GQA_KV_HEADS, HEAD_DIM), 1.0),
        'cache_v1': nrm((DEC_BATCH, PAST_LEN, GQA_KV_HEADS, HEAD_DIM), 1.0),
        'c': nrm((DEC_BATCH, D), 1.0),
        'c_ctx': nrm((D,), 1.0),
        'l0_mod_w': nrm((D, N_MOD * D), 0.3 * s),
        'l0_mod_b': nrm((N_MOD * D,), 0.02),
        'l0_norm_g': 1.0 + nrm((4, D), 0.02),
        'l0_w_in': nrm((D, 2 * CONV_CH + 3 * DIFF_WIDTH), s),
        'l0_conv_w': nrm((CONV_WIDTH, 1, CONV_CH), CONV_WIDTH ** -0.5),
        'l0_conv_b': nrm((CONV_CH,), 0.02),
        'l0_conv_ln_g': 1.0 + nrm((CONV_CH,), 0.02),
        'l0_conv_ln_b': nrm((CONV_CH,), 0.02),
        'l0_lambda': nrm((4, DIFF_SUB_DIM), 0.1),
        'l0_subln_g': 1.0 + nrm((DIFF_V_DIM,), 0.02),
        'l0_w_out': nrm((CONV_CH + DIFF_WIDTH, D), (CONV_CH + DIFF_WIDTH) ** -0.5),
        'l0_w_gu': nrm((D, 2 * FFN_HIDDEN), s),
        'l0_w_down': nrm((FFN_HIDDEN, D), FFN_HIDDEN ** -0.5),
        'l1_mod_w': nrm((D, N_MOD * D), 0.3 * s),
        'l1_mod_b': nrm((N_MOD * D,), 0.02),
        'l1_norm_g': 1.0 + nrm((4, D), 0.02),
        'l1_w_qkv': nrm((D, (GQA_HEADS + 2 * GQA_KV_HEADS) * HEAD_DIM), s),
        'l1_sink': nrm((GQA_HEADS,), 0.5),
        'l1_w_out': nrm((GQA_HEADS * HEAD_DIM, D), (GQA_HEADS * HEAD_DIM) ** -0.5),
        'l1_w_gu': nrm((D, 2 * FFN_HIDDEN), s),
        'l1_w_down': nrm((FFN_HIDDEN, D), FFN_HIDDEN ** -0.5),
    }


def reference(x_prompt, x_sample, cache_k0, cache_v0, cache_k1, cache_v1, c, c_ctx,
              l0_mod_w, l0_mod_b, l0_norm_g, l0_w_in, l0_conv_w, l0_conv_b, l0_conv_ln_g,
              l0_conv_ln_b, l0_lambda, l0_subln_g, l0_w_out, l0_w_gu, l0_w_down,
              l1_mod_w, l1_mod_b, l1_norm_g, l1_w_qkv, l1_sink, l1_w_out, l1_w_gu, l1_w_down):
    common = ((l0_mod_w, l0_mod_b, l0_norm_g, l0_w_gu, l0_w_down),
              (l1_mod_w, l1_mod_b, l1_norm_g, l1_w_gu, l1_w_down))
    mix = ((l0_w_in, l0_conv_w, l0_conv_b, l0_conv_ln_g, l0_conv_ln_b, l0_lambda, l0_subln_g, l0_w_out),
           (l1_w_qkv, l1_sink, l1_w_out))
    caches = ((cache_k0, cache_v0), (cache_k1, cache_v1))
    ctx_cond = c_ctx[None, :]
    y_prompt = x_prompt
    new_kv = []
    for layer in range(DEPTH):
        y_prompt, kv = trunk_layer(y_prompt, ctx_cond, layer, common[layer], mix[layer], None)
        new_kv.append(kv)
    y_sample = x_sample
    for layer in range(DEPTH):
        y_sample, _ = trunk_layer(y_sample, c, layer, common[layer], mix[layer], caches[layer])
    (new_k0, new_v0), (new_k1, new_v1) = new_kv
    return (y_prompt, y_sample, new_k0, new_v0, new_k1, new_v1)
```

```python
import math
from contextlib import ExitStack

import numpy as np
import concourse.bass as bass
import concourse.mybir as mybir
from concourse.bass_utils import run_bass_kernel_spmd

F32 = mybir.dt.float32
BF16 = mybir.dt.bfloat16
AF = mybir.ActivationFunctionType
ALU = mybir.AluOpType
AX = mybir.AxisListType

D = 1024
KC = 8
FF = 2816
JC = 22
EPS = 1e-6
NCORES = 8
NWB = 4
WBN = 4096
ROPE_THETA = 10000.0


class Sem:
    def __init__(self, h, name):
        self.h = h
        self.name = name


class Buf:
    __slots__ = ("name", "w", "rs", "dsem", "dcnt", "excl")

    def __init__(self, name, excl=False):
        self.name = name
        self.excl = excl
        self.w = None
        self.rs = {}
        self.dsem = None
        self.dcnt = 0


class Eng:
    def __init__(self, name, sem):
        self.name = name
        self.sem = sem
        self.count = 0
        self.known = {}
        self.ops = []


class Sched:
    def __init__(self, nc, stack):
        self.nc = nc
        self.stack = stack
        self.eng = {}
        for n in ("pe", "act", "dve", "pool", "sp"):
            self.eng[n] = Eng(n, self.new_sem("e_" + n))
        self.final = {}

    def new_sem(self, name):
        h = self.stack.enter_context(self.nc.semaphore(name))
        return Sem(h, name)

    def _needs(self, eng, reads, writes):
        need = {}

        def req(ev):
            if ev is None:
                return
            sem, val = ev
            if sem is eng.sem and eng.name == "pe":
                return
            if eng.known.get(sem, 0) >= val:
                return
            if need.get(sem, 0) < val:
                need[sem] = val

        for b in reads:
            req(b.w)
            if b.excl:
                for s, v in b.rs.items():
                    if s is not eng.sem:
                        req((s, v))
        for b in writes:
            req(b.w)
            for s, v in b.rs.items():
                req((s, v))
        for s, v in need.items():
            eng.known[s] = v
        return list(need.items())

    def op(self, en, fn, reads=(), writes=(), inc=True):
        eng = self.eng[en]
        waits = self._needs(eng, reads, writes)
        if inc:
            eng.count += 1
            ev = (eng.sem, eng.count)
        else:
            ev = (eng.sem, eng.count + 1)
        eng.ops.append((waits, fn, 1 if inc else 0, eng.sem))
        for b in reads:
            if b.rs.get(ev[0], 0) < ev[1]:
                b.rs[ev[0]] = ev[1]
        for b in writes:
            b.w = ev
            b.rs = {}

    def dma(self, qn, out_ap, in_ap, reads=(), writes=(), final=False, track=None):
        eng = self.eng[qn]
        tb = track if track is not None else (list(writes) + list(reads))[0]
        wr = [b for b in writes if not (b.w is not None and b.w[0] is tb.dsem and not b.rs)]
        waits = self._needs(eng, reads, wr)
        if tb.dsem is None:
            tb.dsem = self.new_sem("d_" + tb.name)
        tb.dcnt += 16
        ev = (tb.dsem, tb.dcnt)
        eng.ops.append((waits, (lambda e, o=out_ap, i=in_ap: e.dma_start(out=o, in_=i)), 16, tb.dsem))
        for b in reads:
            if b.rs.get(ev[0], 0) < ev[1]:
                b.rs[ev[0]] = ev[1]
        for b in writes:
            b.w = ev
            b.rs = {}
        if final:
            if self.final.get(ev[0], 0) < ev[1]:
                self.final[ev[0]] = ev[1]
        return ev

    def replay(self, en, e):
        eng = self.eng[en]
        for waits, fn, inc, sem in eng.ops:
            for s, v in waits:
                e.wait_ge(s.h, v)
            ins = fn(e)
            if inc:
                ins.then_inc(sem.h, inc)


def handoff(old_bufs, new_bufs):
    merged = {}
    for b in old_bufs:
        if b.w is not None:
            s, v = b.w
            if merged.get(s, 0) < v:
                merged[s] = v
        for s, v in b.rs.items():
            if merged.get(s, 0) < v:
                merged[s] = v
    for b in new_bufs:
        b.w = None
        b.rs = dict(merged)


class Ring:
    def __init__(self, items):
        self.items = items
        self.i = 0

    def __call__(self):
        it = self.items[self.i % len(self.items)]
        self.i += 1
        return it


class TB:
    __slots__ = ("ap", "buf")

    def __init__(self, ap, buf):
        self.ap = ap
        self.buf = buf


class StopBuild(Exception):
    pass


def build_program(stop_at=None):
    import os
    stop_at = stop_at or os.environ.get("MK_STOP_AT")

    def ckpt(name):
        if stop_at is not None and name == stop_at:
            raise StopBuild(name)

    nc = bass.Bass("TRN2", target_bir_lowering=False)
    stack = ExitStack()
    S = Sched(nc, stack)

    def din(name, shape):
        return nc.dram_tensor(name, list(shape), F32, kind="ExternalInput").ap()

    def dout(name, shape):
        return nc.dram_tensor(name, list(shape), F32, kind="ExternalOutput").ap()

    xp_d = din("xp", [D, 1024])
    xsa_d = din("xsa", [D, 1024])
    xse_d = din("xse", [D, 542])
    ck0T_d = din("ck0T", [512, 256])
    cv0_d = din("cv0", [256, 512])
    ck1T_d = din("ck1T", [512, 256])
    cv1_d = din("cv1", [256, 256])
    condT_d = din("condT", [128, 16])
    modw_d = [din("modw0", [D, 6 * D]), din("modw1", [D, 6 * D])]
    modb_d = [din("modb0", [128, 96]), din("modb1", [128, 96])]
    gn_d = [din("gn0", [128, 32]), din("gn1", [128, 32])]
    win_d = din("w_in", [D, 2560])
    wqkv_d = din("w_qkv", [D, 1536])
    wout_d = [din("w_out0", [D, D]), din("w_out1", [D, D])]
    wgu_d = [din("w_gu0", [D, 2 * FF]), din("w_gu1", [D, 2 * FF])]
    wdn_d = [din("w_down0", [FF, D]), din("w_down1", [FF, D])]
    convw_d = din("convw", [128, 124])
    convb_d = din("convb", [128, 4])
    lng_d = din("lng", [128, 4])
    lnb_d = din("lnb", [128, 4])
    lam_d = din("lam", [128, 256])
    subg_d = din("subg", [128, 1])
    sink_d = din("sink", [128, 8])
    ropeAc_d = din("ropeAc", [128, 1024])
    ropeAs_d = din("ropeAs", [128, 1024])
    ropeEc_d = din("ropeEc", [128, 512])
    ropeEs_d = din("ropeEs", [128, 512])
    valid_d = din("valid", [128, 542])
    mask1_d = din("mask1", [128, 1024])
    ident_d = din("ident", [128, 128])
    perm_d = din("perm", [128, 128])

    yT_d = dout("yT", [D, 1280])
    k0T_d = dout("k0T", [512, 1024])
    v0_d = dout("v0", [1024, 512])
    k1T_d = dout("k1T", [256, 1024])
    v1_d = dout("v1", [1024, 256])

    def sb(name, shape, dt):
        return stack.enter_context(nc.sbuf_tensor("s_" + name, list(shape), dt))

    xT = sb("xT", [128, 8, 2, 512], F32)
    hT = sb("hT", [128, 8, 2, 512], BF16)
    M = sb("M", [128, 8, 2, 512], F32)
    big = sb("big", [128, JC * 1024], BF16)
    wb = sb("wb", [128, NWB, WBN], BF16)
    sqr = sb("sqr", [128, 3, 512], BF16)
    rstd_t = sb("rstd", [128, 2, 512], F32)
    tmp_t = sb("tmp", [128, 5, 512], F32)
    stg_t = sb("stg", [128, 3, 512], F32)
    xb_t = sb("xb", [128, 2, 512], BF16)
    pbig = sb("pbig", [128, 2, 1536], BF16)
    diag_t = sb("diag", [128, 1, 31, 128], BF16)
    mean_t = sb("mean", [128, 1, 512], F32)
    ones_t = sb("ones", [128, 128], BF16)
    ident_t = sb("identb", [128, 128], BF16)
    perm_t = sb("permb", [128, 128], BF16)
    condT_t = sb("condT", [128, 16], F32)
    sc_t = sb("sc", [128, 16], BF16)
    mpar = [sb("mpar0", [128, 96], F32), sb("mpar1", [128, 96], F32)]
    modb_t = [sb("modbt0", [128, 96], F32), sb("modbt1", [128, 96], F32)]
    gn_t = [sb("gnt0", [128, 32], F32), sb("gnt1", [128, 32], F32)]
    convw_t = sb("convwt", [128, 124], F32)
    convb_t = sb("convbt", [128, 4], F32)
    lng_t = sb("lngt", [128, 4], F32)
    lnb_t = sb("lnbt", [128, 4], F32)
    lam_t = mean_t[:, 0, 0:256]
    lamw = sb("lamw", [128, 8], F32)
    subg_t = sb("subgt", [128, 1], F32)
    sink_t = sb("sinkt", [128, 8], F32)
    ropeEc_t = sb("ropeEc", [128, 512], F32)
    ropeEs_t = sb("ropeEs", [128, 512], F32)
    valid_t = sb("validt", [128, 542], BF16)
    mask1_t = sb("mask1t", [128, 1024], BF16)

    banks = [stack.enter_context(nc.psum_tensor("ps%d" % i, [128, 512], F32)) for i in range(8)]
    bankb = [Buf("bank%d" % i, excl=True) for i in range(8)]

    xTb = [[Buf("xT%d_%d" % (c, g)) for g in range(2)] for c in range(8)]
    hTb = [[Buf("hT%d_%d" % (c, g)) for g in range(2)] for c in range(8)]
    Mb = [[Buf("M%d_%d" % (c, g)) for g in range(2)] for c in range(8)]
    hidb = [[Buf("hid%d_%d" % (j, g)) for g in range(2)] for j in range(JC)]
    wbb = [Buf("wb%d" % i) for i in range(NWB)]
    constb = Buf("const")

    sq_ring = Ring([TB(sqr[:, i, :], Buf("sq%d" % i)) for i in range(3)])
    rstd_ring = Ring([TB(rstd_t[:, i, :], Buf("rstd%d" % i)) for i in range(2)])
    tmp_ring = Ring([TB(tmp_t[:, i, :], Buf("tmp%d" % i)) for i in range(5)])
    stg_ring = Ring([TB(stg_t[:, i, :], Buf("stg%d" % i)) for i in range(3)])
    xb_ring = Ring([TB(xb_t[:, i, :], Buf("xb%d" % i)) for i in range(2)])
    pflat = pbig[:, :, :].rearrange("p a b -> p (a b)")
    p6_ring = Ring([TB(pflat[:, i * 512:(i + 1) * 512], Buf("p6_%d" % i)) for i in range(6)])
    pair_ring = Ring([TB(pflat[:, i * 1024:(i + 1) * 1024], Buf("pp_%d" % i)) for i in range(3)])
    diag_ring = Ring([TB(diag_t[:, i, :, :], Buf("diag%d" % i)) for i in range(1)])
    mean_ring = Ring([TB(mean_t[:, i, :], Buf("mean%d" % i)) for i in range(1)])
    rr_state = {"banks": list(range(8)), "i": 0}

    def nb():
        lst = rr_state["banks"]
        i = lst[rr_state["i"] % len(lst)]
        rr_state["i"] += 1
        return TB(banks[i], bankb[i])

    def bank(i):
        return TB(banks[i], bankb[i])

    def set_rr(lst):
        rr_state["banks"] = list(lst)
        rr_state["i"] = 0

    def ACT(out, in_, func, reads, writes, scale=1.0, bias=0.0):
        S.op("act", lambda e, o=out, i=in_, f=func, s=scale, b=bias: e.activation(out=o, in_=i, func=f, bias=b, scale=s),
             reads, writes)

    def TT(out, a, b, op, reads, writes):
        S.op("dve", lambda e, o=out, a=a, b=b, op=op: e.tensor_tensor(o, a, b, op), reads, writes)

    def TS(out, a, s1, s2, op0, op1, reads, writes):
        if s2 is None:
            S.op("dve", lambda e, o=out, a=a, s1=s1, op0=op0: e.tensor_scalar(o, a, s1, None, op0), reads, writes)
        else:
            S.op("dve", lambda e, o=out, a=a, s1=s1, s2=s2, op0=op0, op1=op1: e.tensor_scalar(o, a, s1, s2, op0, op1),
                 reads, writes)

    def STT(out, a, scalar, b, op0, op1, reads, writes):
        S.op("dve", lambda e, o=out, a=a, sc=scalar, b=b, op0=op0, op1=op1: e.scalar_tensor_tensor(o, a, sc, b, op0, op1),
             reads, writes)

    def RECIP(out, a, reads, writes):
        S.op("dve", lambda e, o=out, a=a: e.reciprocal(o, a), reads, writes)

    def COPYV(out, a, reads, writes):
        S.op("dve", lambda e, o=out, a=a: e.tensor_copy(o, a), reads, writes)

    def MM(out, lhsT, rhs, start, stop, reads, writes, inc):
        S.op("pe", lambda e, o=out, l=lhsT, r=rhs, st=start, sp=stop: e.matmul(o, l, r, start=st, stop=sp),
             reads, writes, inc=True)

    def mm_group(out_ap, out_buf, items):
        n = len(items)
        for i, (l, r, rd) in enumerate(items):
            MM(out_ap, l, r, i == 0, i == n - 1, rd, [out_buf], inc=(i == n - 1))

    cev = []

    def cload(q, dst_ap, src_ap):
        cev.append(S.dma(q, dst_ap, src_ap, writes=[constb]))

    cload("sp", condT_t[:, :], condT_d[:, :])
    for l in range(2):
        cload("sp", modb_t[l][:, :], modb_d[l][:, :])
        cload("sp", gn_t[l][:, :], gn_d[l][:, :])
    cload("sp", convw_t[:, :], convw_d[:, :])
    cload("sp", convb_t[:, :], convb_d[:, :])
    cload("sp", lng_t[:, :], lng_d[:, :])
    cload("sp", lnb_t[:, :], lnb_d[:, :])
    cload("sp", lam_t, lam_d[:, :])
    cload("sp", subg_t[:, :], subg_d[:, :])
    cload("sp", sink_t[:, :], sink_d[:, :])
    cload("sp", ropeEc_t[:, :], ropeEc_d[:, :])
    cload("sp", ropeEs_t[:, :], ropeEs_d[:, :])
    constb2 = Buf("const2")
    S.dma("pool", ident_t[:, :], ident_d[:, :], writes=[constb2])
    S.dma("pool", perm_t[:, :], perm_d[:, :], writes=[constb2])
    S.dma("pool", valid_t[:, :], valid_d[:, :], writes=[constb2])
    S.dma("pool", mask1_t[:, :], mask1_d[:, :], writes=[constb2])
    CB = [constb, constb2]

    S.op("dve", lambda e: e.memset(ones_t[:, :], 1.0), [], [constb2])

    plan = []
    wstate = {"issued": 0, "pos": 0}

    def wview(bi, a, b):
        return wb[:, bi, 0:a * b].rearrange("p (a b) -> p a b", b=b)

    def add_piece(tag, parts):
        plan.append((tag, parts))

    def issue_piece(idx):
        tag, parts = plan[idx]
        bi = idx % NWB
        for (kch, tot, off, n, src) in parts:
            v = wview(bi, kch, tot)
            S.dma("pool", v[:, :, off:off + n], src, writes=[wbb[bi]])

    def next_piece(tag, hold=0):
        pos = wstate["pos"]
        while wstate["issued"] < min(len(plan), pos + NWB - hold):
            issue_piece(wstate["issued"])
            wstate["issued"] += 1
        ptag, parts = plan[pos]
        assert ptag == tag, (ptag, tag)
        wstate["pos"] += 1
        return pos % NWB

    def wsrc(w_ap, c0, n):
        return w_ap[:, c0:c0 + n].rearrange("(c p) n -> p c n", p=128)

    def plan_mod_piece(l, i):
        add_piece("mod%d_%d" % (l, i), [(8, 512, 0, 512, wsrc(modw_d[l], i * 512, 512))])

    def plan_mod(l, part):
        p0, p1 = (0, 4) if part == 0 else (4, 12)
        for i in range(p0, p1):
            add_piece("mod%d_%d" % (l, i), [(8, 512, 0, 512, wsrc(modw_d[l], i * 512, 512))])

    def plan_layer(kind, l, with_mod=False, first_mod=True, next_mod=None):
        if with_mod and first_mod:
            plan_mod(l, 0)
        plan_layer_in(kind, l)
        if with_mod:
            plan_mod(l, 1)
        plan_layer_rest(kind, l, next_mod)

    def plan_layer_in(kind, l):
        if l == 0:
            if kind == "sample":
                add_piece("k0", [(8, 512, 0, 512, wsrc(win_d, 1536, 512))])
                add_piece("v0", [(8, 512, 0, 512, wsrc(win_d, 2048, 512))])
            for i in range(2):
                add_piece("glu%d" % i, [(8, 512, 0, 256, wsrc(win_d, i * 256, 256)),
                                        (8, 512, 256, 256, wsrc(win_d, 512 + i * 256, 256))])
            add_piece("q0", [(8, 512, 0, 512, wsrc(win_d, 1024, 512))])
            if kind == "prompt":
                add_piece("k0", [(8, 512, 0, 512, wsrc(win_d, 1536, 512))])
                add_piece("v0", [(8, 512, 0, 512, wsrc(win_d, 2048, 512))])
        else:
            for i in range(2):
                add_piece("q1_%d" % i, [(8, 512, 0, 512, wsrc(wqkv_d, i * 512, 512))])
            parts = []
            for kv in range(4):
                for dup in range(2):
                    parts.append((8, 512, kv * 128 + dup * 64, 64, wsrc(wqkv_d, 1024 + kv * 64, 64)))
            add_piece("k1", parts)
            add_piece("v1", [(8, 256, 0, 256, wsrc(wqkv_d, 1280, 256))])

    def plan_layer_rest(kind, l, next_mod=None):
        for i in range(2):
            add_piece("out%d_%d" % (l, i), [(8, 512, 0, 512, wsrc(wout_d[l], i * 512, 512))])
        for i in range(11):
            if next_mod is not None and i in (3, 5, 7, 9):
                plan_mod_piece(next_mod, (3, 5, 7, 9).index(i))
            add_piece("gu%d_%d" % (l, i), [(8, 512, 0, 256, wsrc(wgu_d[l], i * 256, 256)),
                                           (8, 512, 256, 256, wsrc(wgu_d[l], FF + i * 256, 256))])
        for c in range(8):
            add_piece("dn%d_%d" % (l, c), [(JC, 128, 0, 128,
                                            wdn_d[l][:, c * 128:(c + 1) * 128].rearrange("(j p) n -> p j n", p=128))])

    sample_first = bool(os.environ.get("MK_SAMPLE_FIRST"))
    if sample_first:
        plan_layer("sample", 0, True)
        plan_layer("sample", 1, True)
        plan_layer("prompt", 0)
        plan_layer("prompt", 1)
    else:
        plan_layer("prompt", 0, True, next_mod=1)
        plan_layer("prompt", 1, True, first_mod=False)
        plan_layer("sample", 0)
        plan_layer("sample", 1)

    scb = Buf("scb")
    ACT(sc_t[:, :], condT_t[:, :], AF.Silu, CB, [scb])

    LAM_INIT0 = 0.8 - 0.6 * math.exp(0.0)
    lamb = Buf("lamb")
    t1 = tmp_ring()
    TT(t1.ap[:, 0:64], lam_t[:, 0:64], lam_t[:, 64:128], ALU.mult, CB + [mean_ring.items[0].buf], [t1.buf])
    TT(t1.ap[:, 64:128], lam_t[:, 128:192], lam_t[:, 192:256], ALU.mult, CB + [mean_ring.items[0].buf], [t1.buf])
    S.op("dve", lambda e: e.reduce_sum(lamw[:, 0:1], t1.ap[:, 0:64], AX.X), [t1.buf], [lamb])
    S.op("dve", lambda e: e.reduce_sum(lamw[:, 1:2], t1.ap[:, 64:128], AX.X), [t1.buf], [lamb])
    ACT(lamw[:, 2:4], lamw[:, 0:2], AF.Exp, [lamb], [lamb])
    TT(lamw[:, 4:5], lamw[:, 3:4], lamw[:, 2:3], ALU.subtract, [lamb], [lamb])
    TS(lamw[:, 5:6], lamw[:, 4:5], -LAM_INIT0, None, ALU.add, None, [lamb], [lamb])
    TS(lamw[:, 6:7], subg_t[:, 0:1], 1.0 - LAM_INIT0, None, ALU.mult, None, [lamb] + CB, [lamb])
    neglam = lamw[:, 5:6]
    sgs = lamw[:, 6:7]
    sinkb = Buf("sinkb")
    ACT(sink_t[:, :], sink_t[:, :], AF.Exp, CB, [sinkb])

    mparbA = [Buf("mparA0"), Buf("mparA1")]
    mparbB = [Buf("mparB0"), Buf("mparB1")]

    def parbuf(l, n):
        return mparbA[l] if n < 2 else mparbB[l]

    def do_mod(l, part, bk=None):
        for _ in mod_steps(l, part, bk):
            pass

    def mod_steps(l, part, bk=None):
        if bk is None:
            bk = nb()
        p0, p1 = (0, 4) if part == 0 else (4, 12)
        pb = mparbA[l] if part == 0 else mparbB[l]
        for i in range(p0, p1):
            bi = next_piece("mod%d_%d" % (l, i))
            v = wview(bi, 8, 512)
            for fc in range(4):
                col = (i * 4 + fc) * 2
                items = [(v[:, k, fc * 128:(fc + 1) * 128], sc_t[:, k * 2:k * 2 + 2], [wbb[bi], scb]) for k in range(8)]
                mm_group(bk.ap[:, col:col + 2], bk.buf, items)
            if i < p1 - 1:
                yield i
        mp = mpar[l]
        c0, c1 = p0 * 8, p1 * 8
        TT(mp[:, c0:c1], bk.ap[:, c0:c1], modb_t[l][:, c0:c1], ALU.add, [bk.buf] + CB, [pb])
        m3 = mp[:, :].rearrange("p (n c i) -> p n c i", n=6, c=8, i=2)
        g = gn_t[l]
        for ci in range(2):
            if part == 0:
                STT(m3[:, 1, :, ci], m3[:, 1, :, ci], 1.0, g[:, 0:8], ALU.add, ALU.mult, [pb] + CB, [pb])
            else:
                TT(m3[:, 2, :, ci], m3[:, 2, :, ci], g[:, 8:16], ALU.mult, [pb] + CB, [pb])
                STT(m3[:, 4, :, ci], m3[:, 4, :, ci], 1.0, g[:, 16:24], ALU.add, ALU.mult, [pb] + CB, [pb])
                TT(m3[:, 5, :, ci], m3[:, 5, :, ci], g[:, 24:32], ALU.mult, [pb] + CB, [pb])
        yield -1

    def par(l, n, c, ci):
        i = (n * 8 + c) * 2 + ci
        return mpar[l][:, i:i + 1]

    def flat(t, c):
        return t[:, c, :, :].rearrange("p g t -> p (g t)")

    def colbufs(bufs2d, c, c0, c1):
        gs = sorted(set([c0 // 512, (c1 - 1) // 512]))
        return [bufs2d[c][g] for g in gs]

    def rstd_from(bk_ap, bk_buf, W, n, eps):
        t = tmp_ring()
        r = rstd_ring()
        ACT(t.ap[:, 0:W], bk_ap, AF.Ln, [bk_buf], [t.buf], scale=1.0 / n, bias=eps)
        ACT(r.ap[:, 0:W], t.ap[:, 0:W], AF.Exp, [t.buf], [r.buf], scale=-0.5)
        return r

    def sumsq(srcs, W):
        bk = nb()
        n = len(srcs)
        for i, (ap, bufs) in enumerate(srcs):
            s = sq_ring()
            ACT(s.ap[:, 0:W], ap, AF.Square, bufs, [s.buf])
            MM(bk.ap[:, 0:W], ones_t[:, :], s.ap[:, 0:W], i == 0, i == n - 1, [s.buf, constb2], [bk.buf], inc=(i == n - 1))
        return bk

    NWARM = [int(x) for x in os.environ.get("MK_WARM", "0,0").split(",")]

    def warm(n):
        if n <= 0:
            return
        bk = nb()
        for i in range(n):
            MM(bk.ap[:, :], ones_t[:, :], mask1_t[:, 0:512], True, True, [constb2], [bk.buf], inc=True)

    def prenorm(l, ci, nA, nB, src_t, src_b, s0, W, d0):
        srcs = [(flat(src_t, c)[:, s0:s0 + W], colbufs(src_b, c, s0, s0 + W)) for c in range(8)]
        bk = sumsq(srcs, W)
        warm(NWARM[0])
        r = rstd_from(bk.ap[:, 0:W], bk.buf, W, 1024.0, EPS)
        for c in range(8):
            t = tmp_ring()
            STT(t.ap[:, 0:W], srcs[c][0], par(l, nA, c, ci), r.ap[:, 0:W], ALU.mult, ALU.mult,
                srcs[c][1] + [r.buf, parbuf(l, nA)], [t.buf])
            ACT(flat(hT, c)[:, d0:d0 + W], t.ap[:, 0:W], AF.Identity, [t.buf, parbuf(l, nB)], colbufs(hTb, c, d0, d0 + W),
                bias=par(l, nB, c, ci))

    POOL_ADD = True

    class EvacStats:
        def __init__(self, gs, banks_):
            self.bank = {g: b for g, b in zip(gs, banks_)}
            self.pend = None

        def add(self, bk, c, g, W):
            sq = sq_ring()
            ACT(sq.ap[:, 0:W], bk.ap[:, 0:W], AF.Square, [bk.buf], [sq.buf])
            self.flush()
            self.pend = (sq, c, g, W)

        def flush(self):
            if self.pend is None:
                return
            sq, c, g, W = self.pend
            sb_ = self.bank[g]
            MM(sb_.ap[:, 0:W], ones_t[:, :], sq.ap[:, 0:W], c == 0, c == 7, [sq.buf, constb2], [sb_.buf], inc=True)
            self.pend = None

    def postnorm(l, ci, nG, g, m0, W, x0, eps, stat=None):
        srcs = [(M[:, c, g, m0:m0 + W], [Mb[c][g]]) for c in range(8)]
        bk = stat if stat is not None else sumsq(srcs, W)
        warm(NWARM[1])
        r = rstd_from(bk.ap[:, 0:W], bk.buf, W, 1024.0, eps)
        for c in range(8):
            t = tmp_ring()
            STT(t.ap[:, 0:W], srcs[c][0], par(l, nG, c, ci), r.ap[:, 0:W], ALU.mult, ALU.mult,
                [Mb[c][g], r.buf, parbuf(l, nG)], [t.buf])
            if POOL_ADD and c % 2 == 1:
                S.op("pool", lambda e, o=xT[:, c, g, x0:x0 + W], a=xT[:, c, g, x0:x0 + W], b=t.ap[:, 0:W]: e.tensor_tensor(o, a, b, ALU.add),
                     [t.buf, xTb[c][g]], [xTb[c][g]])
            else:
                TT(xT[:, c, g, x0:x0 + W], xT[:, c, g, x0:x0 + W], t.ap[:, 0:W], ALU.add, [t.buf, xTb[c][g]], [xTb[c][g]])

    def proj_to_M(tagfmt, l, nchunks_per_piece, src_t, src_b, groups, after_group=None):
        if len(groups) > 1 and nchunks_per_piece == 4:
            b0 = next_piece(tagfmt % (l, 0))
            b1 = next_piece(tagfmt % (l, 1), hold=1)
            vs = [wview(b0, 8, 512), wview(b1, 8, 512)]
            bis = [b0, b1]
            set_rr(range(6))
            es = EvacStats([g for (g, s0, W) in groups], [bank(6), bank(7)])
            for (g, s0, W) in groups:
                for c in range(8):
                    v, bi, cc = vs[c // 4], bis[c // 4], c % 4
                    bk = nb()
                    items = [(v[:, k, cc * 128:(cc + 1) * 128], src_t[:, k, g, s0:s0 + W], [wbb[bi], src_b[k][g]]) for k in range(8)]
                    mm_group(bk.ap[:, 0:W], bk.buf, items)
                    ACT(M[:, c, g, 0:W], bk.ap[:, 0:W], AF.Copy, [bk.buf], [Mb[c][g]])
                    es.add(bk, c, g, W)
                es.flush()
                if after_group is not None:
                    after_group(g, es.bank[g])
            set_rr(range(8))
            return
        set_rr(range(6))
        es = EvacStats([g for (g, s0, W) in groups], [bank(6), bank(7)])
        for c in range(8):
            if c % nchunks_per_piece == 0:
                bi = next_piece(tagfmt % (l, c // nchunks_per_piece))
                v = wview(bi, 8, 512)
            cc = c % nchunks_per_piece
            for (g, s0, W) in groups:
                bk = nb()
                items = [(v[:, k, cc * 128:(cc + 1) * 128], src_t[:, k, g, s0:s0 + W], [wbb[bi], src_b[k][g]]) for k in range(8)]
                mm_group(bk.ap[:, 0:W], bk.buf, items)
                ACT(M[:, c, g, 0:W], bk.ap[:, 0:W], AF.Copy, [bk.buf], [Mb[c][g]])
                es.add(bk, c, g, W)
        es.flush()
        set_rr(range(8))
        return es

    def ffn(l, ci, groups, xoff, mid_hook=None, after_group=None):
        mh = None
        if mid_hook is not None:
            set_rr(range(7))
            mh = mid_hook(bank(7))
        for i in range(11):
            if mh is not None and i in (3, 5, 7, 9):
                next(mh, None)
            bi = next_piece("gu%d_%d" % (l, i))
            v = wview(bi, 8, 512)
            for (g, W) in groups:
                for jj in range(2):
                    j = i * 2 + jj
                    bg = nb()
                    items = [(v[:, k, jj * 128:(jj + 1) * 128], hT[:, k, g, 0:W], [wbb[bi], hTb[k][g]]) for k in range(8)]
                    mm_group(bg.ap[:, 0:W], bg.buf, items)
                    bu = nb()
                    items = [(v[:, k, 256 + jj * 128:256 + (jj + 1) * 128], hT[:, k, g, 0:W], [wbb[bi], hTb[k][g]]) for k in range(8)]
                    mm_group(bu.ap[:, 0:W], bu.buf, items)
                    t = tmp_ring()
                    ACT(t.ap[:, 0:W], bg.ap[:, 0:W], AF.Silu, [bg.buf], [t.buf])
                    hv = big[:, j * 1024 + g * 512: j * 1024 + g * 512 + W]
                    TT(hv, t.ap[:, 0:W], bu.ap[:, 0:W], ALU.mult, [t.buf, bu.buf], [hidb[j][g]])
        set_rr(range(6))
        es = EvacStats([g for (g, W) in groups], [bank(6), bank(7)])
        for c in range(8):
            bi = next_piece("dn%d_%d" % (l, c))
            v = wview(bi, JC, 128)
            for (g, W) in groups:
                bk = nb()
                items = [(v[:, j, :], big[:, j * 1024 + g * 512: j * 1024 + g * 512 + W], [wbb[bi], hidb[j][g]]) for j in range(JC)]
                mm_group(bk.ap[:, 0:W], bk.buf, items)
                ACT(M[:, c, g, 0:W], bk.ap[:, 0:W], AF.Copy, [bk.buf], [Mb[c][g]])
                es.add(bk, c, g, W)
        es.flush()
        set_rr(range(8))
        for (g, W) in groups:
            postnorm(l, ci, 5, g, 0, W, xoff, EPS, stat=es.bank[g])
            if after_group is not None:
                after_group(g)

    def exp_recip_mul(dst_ap, dst_bufs, v_ap, v_bufs, W):
        t = tmp_ring()
        ACT(t.ap[:, 0:W], v_ap, AF.Exp, v_bufs, [t.buf], scale=-1.0)
        TS(t.ap[:, 0:W], t.ap[:, 0:W], 1.0, None, ALU.add, None, [t.buf], [t.buf])
        RECIP(t.ap[:, 0:W], t.ap[:, 0:W], [t.buf], [t.buf])
        TT(dst_ap, v_ap, t.ap[:, 0:W], ALU.mult, v_bufs + [t.buf], dst_bufs)

    allhid = [hidb[j][g] for j in range(JC) for g in range(2)]
    zb_v = big[:, 0:4576].rearrange("p (c s t) -> p c s t", c=4, s=4, t=286)
    zs_v = big[:, 0:4 * 542].rearrange("p (c t) -> p c t", c=4)
    q0_v = big[:, 4576:4576 + 4096].rearrange("p (c t) -> p c t", c=4)
    k0_v = big[:, 8672:8672 + 5120].rearrange("p (c t) -> p c t", c=4)
    v0_v = big[:, 13792:13792 + 5120].rearrange("p (t e) -> p t e", t=10)
    q1_v = big[:, 0:8192].rearrange("p (c t) -> p c t", c=8)
    k1_v = big[:, 8192:8192 + 5120].rearrange("p (c t) -> p c t", c=4)
    v1_v = big[:, 13312:13312 + 5120].rearrange("p (t k d e) -> p t k d e", t=10, k=4, d=2)
    zbuf_b = [Buf("z%d" % c) for c in range(4)]
    qb = [Buf("q%d" % c) for c in range(8)]
    kb = [Buf("k%d" % c) for c in range(4)]
    kctxb = Buf("kctx")
    vb = [Buf("v%d" % t) for t in range(10)]
    mixbufs = zbuf_b + qb + kb + [kctxb] + vb

    def rope_stage(bk, W, ctab, stab, dst_ap, dst_bufs):
        xbt = xb_ring()
        ACT(xbt.ap[:, 0:W], bk.ap[:, 0:W], AF.Copy, [bk.buf], [xbt.buf])

        def fin():
            b2 = nb()
            MM(b2.ap[:, 0:W], perm_t[:, :], xbt.ap[:, 0:W], True, True, [xbt.buf, constb2], [b2.buf], inc=True)
            ta = tmp_ring()
            tb_ = tmp_ring()
            TT(ta.ap[:, 0:W], bk.ap[:, 0:W], ctab[0], ALU.mult, [bk.buf] + ctab[1], [ta.buf])
            TT(tb_.ap[:, 0:W], b2.ap[:, 0:W], stab[0], ALU.mult, [b2.buf] + stab[1], [tb_.buf])
            TT(dst_ap, ta.ap[:, 0:W], tb_.ap[:, 0:W], ALU.add, [ta.buf, tb_.buf], dst_bufs)
        return fin

    DGRP = [(0, 8), (8, 16), (16, 24), (24, 31)]
    diag_gb = [Buf("diagg%d" % i) for i in range(4)]

    class DG:
        def __init__(self):
            self.ap = diag_t[:, 0, :, :]

        def buf(self, j):
            return diag_gb[min(j // 8, 3)]

    def conv_build_group(c, gi):
        dg = DG()
        ia = ident_t[:, :]
        j0, j1 = DGRP[gi]
        n = j1 - j0
        wa = convw_t[:, c * 31 + j0:c * 31 + j1]
        in0 = bass.AP(ia.tensor, ia.offset, [list(ia.ap[0]), [0, n], [1, 128]])
        in1 = bass.AP(wa.tensor, wa.offset, [list(wa.ap[0]), [1, n], [0, 128]])
        TT(dg.ap[:, j0:j1, :], in0, in1, ALU.mult, CB, [diag_gb[gi]])

    def conv_build_diag(c):
        for gi in range(4):
            conv_build_group(c, gi)
        return DG()

    LAST_TAP = {7: 0, 15: 1, 23: 2, 30: 3}

    def conv_ln_silu(g, W):
        b1 = nb()
        b2 = nb()
        for c in range(4):
            xbt = xb_ring()
            ACT(xbt.ap[:, 0:W], M[:, c, g, 0:W], AF.Copy, [Mb[c][g]], [xbt.buf])
            MM(b1.ap[:, 0:W], ones_t[:, :], xbt.ap[:, 0:W], c == 0, c == 3, [xbt.buf, constb2], [b1.buf], inc=(c == 3))
        for c in range(4):
            s = sq_ring()
            ACT(s.ap[:, 0:W], M[:, c, g, 0:W], AF.Square, [Mb[c][g]], [s.buf])
            MM(b2.ap[:, 0:W], ones_t[:, :], s.ap[:, 0:W], c == 0, c == 3, [s.buf, constb2], [b2.buf], inc=(c == 3))
        mean = mean_ring()
        TS(mean.ap[:, 0:W], b1.ap[:, 0:W], 1.0 / 512.0, None, ALU.mult, None, [b1.buf], [mean.buf])
        t = tmp_ring()
        TT(t.ap[:, 0:W], mean.ap[:, 0:W], mean.ap[:, 0:W], ALU.mult, [mean.buf], [t.buf])
        var = tmp_ring()
        STT(var.ap[:, 0:W], b2.ap[:, 0:W], 1.0 / 512.0, t.ap[:, 0:W], ALU.mult, ALU.subtract, [b2.buf, t.buf], [var.buf])
        t2 = tmp_ring()
        r = rstd_ring()
        ACT(t2.ap[:, 0:W], var.ap[:, 0:W], AF.Ln, [var.buf], [t2.buf], bias=EPS)
        ACT(r.ap[:, 0:W], t2.ap[:, 0:W], AF.Exp, [t2.buf], [r.buf], scale=-0.5)
        for c in range(4):
            u = tmp_ring()
            TT(u.ap[:, 0:W], M[:, c, g, 0:W], mean.ap[:, 0:W], ALU.subtract, [Mb[c][g], mean.buf], [u.buf])
            TT(u.ap[:, 0:W], u.ap[:, 0:W], r.ap[:, 0:W], ALU.mult, [u.buf, r.buf], [u.buf])
            ACT(hT[:, c, g, 0:W], u.ap[:, 0:W], AF.Silu, [u.buf] + CB, [hTb[c][g]],
                scale=lng_t[:, c:c + 1], bias=lnb_t[:, c:c + 1])

    def subln_out(g, W):
        for h in range(4):
            bk = sumsq([(M[:, 4 + h, g, 0:W], [Mb[4 + h][g]])], W)
            r = rstd_from(bk.ap[:, 0:W], bk.buf, W, 128.0, EPS)
            t = tmp_ring()
            TT(t.ap[:, 0:W], M[:, 4 + h, g, 0:W], r.ap[:, 0:W], ALU.mult, [Mb[4 + h][g], r.buf], [t.buf])
            TS(hT[:, 4 + h, g, 0:W], t.ap[:, 0:W], sgs, None, ALU.mult, None, [t.buf, lamb], [hTb[4 + h][g]])

    def recip_act(dst, src_ap, src_bufs, W, bias=0.0, extra_reads=()):
        ACT(dst.ap[:, 0:W], src_ap, AF.Ln, list(src_bufs) + list(extra_reads), [dst.buf], bias=bias)
        ACT(dst.ap[:, 0:W], dst.ap[:, 0:W], AF.Exp, [dst.buf], [dst.buf], scale=-1.0)

    def glu(ba, bg, W, outs):
        t = tmp_ring()
        ACT(t.ap[:, 0:W], bg.ap[:, 0:W], AF.Sigmoid, [bg.buf], [t.buf])
        for (dst_ap, a0, a1, dbufs, shp) in outs:
            a_ap = ba.ap[:, a0:a1]
            t_ap = t.ap[:, a0:a1]
            if shp is not None:
                a_ap = a_ap.rearrange("p (s t) -> p s t", s=shp)
                t_ap = t_ap.rearrange("p (s t) -> p s t", s=shp)
            TT(dst_ap, a_ap, t_ap, ALU.mult, [ba.buf, t.buf], dbufs)

    def store_y(g_list):
        for (g, x0, W, ycol) in g_list:
            for c in range(8):
                S.dma("sp", yT_d[c * 128:(c + 1) * 128, ycol:ycol + W], xT[:, c, g, x0:x0 + W],
                      reads=[xTb[c][g]], final=True, track=ybuf[ycol // 512])

    ybuf = [Buf("yout%d" % i) for i in range(3)]

    def prompt_layer0(mid=None, fuse_next=False):
        ci = 0
        l = 0
        for g in range(2):
            prenorm(l, ci, 1, 0, xT, xTb, g * 512, 512, g * 512)
        handoff(allhid, mixbufs)
        ckpt("p0_norm")
        S.op("dve", lambda e: e.memset(big[:, 0:4576], 0.0), [], zbuf_b)
        for i in range(2):
            bi = next_piece("glu%d" % i)
            v = wview(bi, 8, 512)
            for g in range(2):
                for cc in range(2):
                    c = i * 2 + cc
                    ba = nb()
                    mm_group(ba.ap[:, :], ba.buf, [(v[:, k, cc * 128:(cc + 1) * 128], hT[:, k, g, :], [wbb[bi], hTb[k][g]]) for k in range(8)])
                    bg = nb()
                    mm_group(bg.ap[:, :], bg.buf, [(v[:, k, 256 + cc * 128:256 + (cc + 1) * 128], hT[:, k, g, :], [wbb[bi], hTb[k][g]]) for k in range(8)])
                    glu(ba, bg, 512, [(zb_v[:, c, 2 * g:2 * g + 2, 15:271], 0, 512, [zbuf_b[c]], 2)])
        ckpt("p0_glu")
        bi = next_piece("q0")
        v = wview(bi, 8, 512)
        for c in range(4):
            for g in range(2):
                bk = nb()
                mm_group(bk.ap[:, :], bk.buf, [(v[:, k, c * 128:(c + 1) * 128], hT[:, k, g, :], [wbb[bi], hTb[k][g]]) for k in range(8)])
                ACT(q0_v[:, c, g * 512:(g + 1) * 512], bk.ap[:, :], AF.Copy, [bk.buf], [qb[c]])
        ckpt("p0_q")
        bi = next_piece("k0")
        v = wview(bi, 8, 512)
        for c in range(4):
            for g in range(2):
                bk = nb()
                mm_group(bk.ap[:, :], bk.buf, [(v[:, k, c * 128:(c + 1) * 128], hT[:, k, g, :], [wbb[bi], hTb[k][g]]) for k in range(8)])
                st = stg_ring()
                ACT(st.ap[:, :], bk.ap[:, :], AF.Copy, [bk.buf], [st.buf])
                S.dma("sp", k0T_d[c * 128:(c + 1) * 128, g * 512:(g + 1) * 512], st.ap[:, :], reads=[st.buf], final=True)
                COPYV(k0_v[:, c, g * 512:(g + 1) * 512], st.ap[:, :], [st.buf], [kb[c]])
        ckpt("p0_k")
        bi = next_piece("v0")
        v = wview(bi, 8, 512)
        for tl in range(8):
            g, t0_ = tl // 4, (tl % 4) * 128
            bk = nb()
            mm_group(bk.ap[:, :], bk.buf, [(hT[:, k, g, t0_:t0_ + 128], v[:, k, :], [wbb[bi], hTb[k][g]]) for k in range(8)])
            st = stg_ring()
            ACT(st.ap[:, :], bk.ap[:, :], AF.Copy, [bk.buf], [st.buf])
            S.dma("sp", v0_d[tl * 128:(tl + 1) * 128, :], st.ap[:, :], reads=[st.buf], final=True)
            COPYV(v0_v[:, tl, :], st.ap[:, :], [st.buf], [vb[tl]])
        ckpt("p0_v")
        mg = mid(bank(1)) if mid is not None else None
        steps = [(h, s) for h in range(4) for s in range(4)]
        s_banks = Ring([bank(i) for i in (2, 3, 4, 5)])
        conv_banks = [bank(0), bank(0)]
        stA = {}

        def att_A(h, s):
            pp = pair_ring()
            ppv = pp.ap.rearrange("p (t q) -> p t q", t=2)
            for j in range(2):
                bs = s_banks()
                for t in range(2):
                    kc0 = s * 256 + t * 128
                    MM(bs.ap[:, t * 256:(t + 1) * 256], k0_v[j * 64:(j + 1) * 64, h, kc0:kc0 + 128],
                       q0_v[j * 64:(j + 1) * 64, h, s * 256:(s + 1) * 256], True, True, [kb[h], qb[h]], [bs.buf], inc=True)
                ACT(ppv[:, :, j * 256:(j + 1) * 256], bs.ap[:, :].rearrange("p (t q) -> p t q", t=2), AF.Exp, [bs.buf], [pp.buf], scale=0.125)
            stA[(h, s)] = pp

        def att_B(h, s):
            g, so = s // 2, (s % 2) * 256
            pp = stA.pop((h, s))
            bo = bank(6)
            bd = bank(7)
            for t in range(2):
                tl = s * 2 + t
                MM(bo.ap[:, :], v0_v[:, tl, h * 128:(h + 1) * 128], pp.ap[:, t * 512:(t + 1) * 512],
                   t == 0, t == 1, [vb[tl], pp.buf], [bo.buf], inc=True)
            for t in range(2):
                MM(bd.ap[:, :], ones_t[:, :], pp.ap[:, t * 512:(t + 1) * 512],
                   t == 0, t == 1, [constb2, pp.buf], [bd.buf], inc=True)
            r = tmp_ring()
            recip_act(r, bd.ap[:, :], [bd.buf], 512)
            on = tmp_ring()
            TT(on.ap[:, :], bo.ap[:, :], r.ap[:, :], ALU.mult, [bo.buf, r.buf], [on.buf])
            STT(M[:, 4 + h, g, so:so + 256], on.ap[:, 256:512], neglam, on.ap[:, 0:256], ALU.mult, ALU.add,
                [on.buf, lamb], [Mb[4 + h][g]])

        dgs = {}

        def conv_step(c, s):
            g, si = s // 2, s % 2
            if s == 0 and c == 0:
                dgs[c] = conv_build_diag(c)
            dg = dgs[c]
            bk = conv_banks[g]
            for j in range(31):
                MM(bk.ap[:, si * 256:(si + 1) * 256], dg.ap[:, j, :], zb_v[:, c, s, j:j + 256], j == 0, j == 30,
                   [dg.buf(j), zbuf_b[c]], [bk.buf], inc=True)
                if s == 3 and c < 3 and j in LAST_TAP:
                    conv_build_group(c + 1, LAST_TAP[j])
                    dgs[c + 1] = DG()
            if si == 1:
                ACT(M[:, c, g, :], bk.ap[:, :], AF.Identity, [bk.buf] + CB, [Mb[c][g]], bias=convb_t[:, c:c + 1])

        att_A(*steps[0])
        for i, (h, s) in enumerate(steps):
            if i + 1 < len(steps):
                att_A(*steps[i + 1])
            conv_step(h, s)
            if mg is not None and i % 2 == 1:
                next(mg, None)
            att_B(h, s)
        if mg is not None:
            for _ in mg:
                pass
        set_rr(range(8))
        ckpt("p0_attn")
        for g in range(2):
            subln_out(g, 512)
            conv_ln_silu(g, 512)
        ckpt("p0_subln")
        def after_out(g, stat):
            postnorm(l, ci, 2, g, 0, 512, 0, EPS, stat=stat)
            prenorm(l, ci, 4, 3, xT, xTb, g * 512, 512, g * 512)

        proj_to_M("out%d_%d", l, 4, hT, hTb, [(0, 0, 512), (1, 0, 512)], after_group=after_out)
        ckpt("p0_out")
        handoff(mixbufs, allhid)
        ckpt("p0_norm2")
        if fuse_next:
            ffn(l, ci, [(0, 512), (1, 512)], 0, mid_hook=lambda bk: mod_steps(1, 0, bk),
                after_group=lambda g: prenorm(1, ci, 1, 0, xT, xTb, g * 512, 512, g * 512))
        else:
            ffn(l, ci, [(0, 512), (1, 512)], 0)

    def prompt_layer1(mid=None, skip_prenorm=False):
        ci = 0
        l = 1
        if not skip_prenorm:
            for g in range(2):
                prenorm(l, ci, 1, 0, xT, xTb, g * 512, 512, g * 512)
        handoff(allhid, mixbufs)
        for i in range(2):
            bi = next_piece("q1_%d" % i)
            v = wview(bi, 8, 512)
            for g in range(2):
                for cc in range(4):
                    c = i * 4 + cc
                    bk = nb()
                    mm_group(bk.ap[:, :], bk.buf, [(v[:, k, cc * 128:(cc + 1) * 128], hT[:, k, g, :], [wbb[bi], hTb[k][g]]) for k in range(8)])
                    ACT(q1_v[:, c, g * 512:(g + 1) * 512], bk.ap[:, :], AF.Copy, [bk.buf], [qb[c]])
        bi = next_piece("k1")
        v = wview(bi, 8, 512)
        for kv in range(4):
            for g in range(2):
                bk = nb()
                mm_group(bk.ap[:, :], bk.buf, [(v[:, k, kv * 128:(kv + 1) * 128], hT[:, k, g, :], [wbb[bi], hTb[k][g]]) for k in range(8)])
                st = stg_ring()
                ACT(st.ap[:, :], bk.ap[:, :], AF.Copy, [bk.buf], [st.buf])
                S.dma("sp", k1T_d[kv * 64:(kv + 1) * 64, g * 512:(g + 1) * 512], st.ap[0:64, :], reads=[st.buf], final=True)
                COPYV(k1_v[:, kv, g * 512:(g + 1) * 512], st.ap[:, :], [st.buf], [kb[kv]])
        bi = next_piece("v1")
        v = wview(bi, 8, 256)
        for tl in range(8):
            g, t0_ = tl // 4, (tl % 4) * 128
            bk = nb()
            mm_group(bk.ap[:, 0:256], bk.buf, [(hT[:, k, g, t0_:t0_ + 128], v[:, k, :], [wbb[bi], hTb[k][g]]) for k in range(8)])
            st = stg_ring()
            ACT(st.ap[:, 0:256], bk.ap[:, 0:256], AF.Copy, [bk.buf], [st.buf])
            S.dma("sp", v1_d[tl * 128:(tl + 1) * 128, :], st.ap[:, 0:256], reads=[st.buf], final=True)
            for dup in range(2):
                COPYV(v1_v[:, tl, :, dup, :], st.ap[:, 0:256].rearrange("p (k e) -> p k e", k=4), [st.buf], [vb[tl]])
        mg = mid(bank(7)) if mid is not None else None
        s_banks = Ring([bank(i) for i in (0, 1, 2)])
        o_banks = Ring([bank(i) for i in (3, 4)])
        d_banks = Ring([bank(i) for i in (5, 6)])
        steps = [(s, c) for s in range(4) for c in range(8)]
        stA = {}

        def att_A(s, c):
            kv = c // 2
            pp = pair_ring()
            ppv = pp.ap.rearrange("p (t q) -> p t q", t=2)
            for hp in range(2):
                pr = slice(hp * 64, (hp + 1) * 64)
                bs = s_banks()
                for t in range(2):
                    kc0 = s * 256 + t * 128
                    MM(bs.ap[:, t * 256:(t + 1) * 256], k1_v[pr, kv, kc0:kc0 + 128], q1_v[pr, c, s * 256:(s + 1) * 256],
                       True, True, [kb[kv], qb[c]], [bs.buf], inc=True)
                ACT(ppv[:, :, hp * 256:(hp + 1) * 256], bs.ap[:, :].rearrange("p (t q) -> p t q", t=2), AF.Exp, [bs.buf], [pp.buf], scale=0.125)
            stA[(s, c)] = pp

        def att_B(s, c):
            g, so = s // 2, (s % 2) * 256
            kv = c // 2
            pp = stA.pop((s, c))
            bo, bd = o_banks(), d_banks()
            for t in range(2):
                tl = s * 2 + t
                MM(bo.ap[:, :], v1_v[:, tl, kv, :, :].rearrange("p d e -> p (d e)"), pp.ap[:, t * 512:(t + 1) * 512],
                   t == 0, t == 1, [vb[tl], pp.buf], [bo.buf], inc=True)
            for t in range(2):
                MM(bd.ap[:, :], ones_t[:, :], pp.ap[:, t * 512:(t + 1) * 512],
                   t == 0, t == 1, [constb2, pp.buf], [bd.buf], inc=True)
            r = tmp_ring()
            for h2 in range(2):
                pr = slice(h2 * 64, (h2 + 1) * 64)
                ACT(r.ap[pr, 0:256], bd.ap[pr, h2 * 256:(h2 + 1) * 256], AF.Ln, [bd.buf, sinkb], [r.buf], bias=sink_t[pr, c:c + 1])
            ACT(r.ap[:, 0:256], r.ap[:, 0:256], AF.Exp, [r.buf], [r.buf], scale=-1.0)
            for h2 in range(2):
                pr = slice(h2 * 64, (h2 + 1) * 64)
                TT(hT[pr, c, g, so:so + 256], bo.ap[pr, h2 * 256:(h2 + 1) * 256], r.ap[pr, 0:256], ALU.mult, [bo.buf, r.buf], [hTb[c][g]])

        att_A(*steps[0])
        for i, st_ in enumerate(steps):
            if i + 1 < len(steps):
                att_A(*steps[i + 1])
            if mg is not None and i % 4 == 3:
                next(mg, None)
            att_B(*st_)
        if mg is not None:
            for _ in mg:
                pass
        set_rr(range(8))

        def after_out(g, stat):
            postnorm(l, ci, 2, g, 0, 512, 0, EPS, stat=stat)
            prenorm(l, ci, 4, 3, xT, xTb, g * 512, 512, g * 512)

        proj_to_M("out%d_%d", l, 4, hT, hTb, [(0, 0, 512), (1, 0, 512)], after_group=after_out)
        handoff(mixbufs, allhid)
        ffn(l, ci, [(0, 512), (1, 512)], 0)

    xsa_loaded = []

    def load_xsa(g):
        S.dma("sp", M[:, :, g, :], xsa_d[:, g * 512:(g + 1) * 512].rearrange("(c p) t -> p c t", p=128),
              writes=[Mb[c][g] for c in range(8)])

    def sample_layer0(mid=None):
        ci = 1
        l = 0
        for g in range(2):
            if not xsa_loaded:
                load_xsa(g)
            prenorm(l, ci, 1, 0, M, Mb, g * 512, 512, g * 512)
        handoff(allhid, mixbufs)
        S.dma("sp", flat(M, 0), ropeAc_d[:, :], writes=[Mb[0][0], Mb[0][1]])
        S.dma("sp", flat(M, 1), ropeAs_d[:, :], writes=[Mb[1][0], Mb[1][1]])
        for c in range(4):
            S.dma("pool", k0_v[:, c, 1024:1280], ck0T_d[c * 128:(c + 1) * 128, :], writes=[kctxb])
        for t in range(2):
            S.dma("pool", v0_v[:, 8 + t, :], cv0_d[t * 128:(t + 1) * 128, :], writes=[vb[8 + t]])
        bi = next_piece("k0")
        v = wview(bi, 8, 512)
        pend = None
        for c in range(4):
            for g in range(2):
                bk = nb()
                mm_group(bk.ap[:, :], bk.buf, [(v[:, k, c * 128:(c + 1) * 128], hT[:, k, g, :], [wbb[bi], hTb[k][g]]) for k in range(8)])
                fin = rope_stage(bk, 512, (M[:, 0, g, :], [Mb[0][g]]), (M[:, 1, g, :], [Mb[1][g]]),
                                 k0_v[:, c, g * 512:(g + 1) * 512], [kb[c]])
                if pend is not None:
                    pend()
                pend = fin
        pend()
        bi = next_piece("v0")
        v = wview(bi, 8, 512)
        for tl in range(8):
            g, t0_ = tl // 4, (tl % 4) * 128
            bk = nb()
            mm_group(bk.ap[:, :], bk.buf, [(hT[:, k, g, t0_:t0_ + 128], v[:, k, :], [wbb[bi], hTb[k][g]]) for k in range(8)])
            ACT(v0_v[:, tl, :], bk.ap[:, :], AF.Copy, [bk.buf], [vb[tl]])
        S.dma("sp", xT[:, :, 0, :], xse_d[:, 0:512].rearrange("(c p) t -> p c t", p=128), writes=[xTb[c][0] for c in range(8)])
        S.dma("sp", xT[:, :, 1, 0:30], xse_d[:, 512:542].rearrange("(c p) t -> p c t", p=128), writes=[xTb[c][1] for c in range(8)])
        prenorm(l, ci, 1, 0, xT, xTb, 0, 512, 0)
        prenorm(l, ci, 1, 0, xT, xTb, 512, 30, 512)
        for i in range(2):
            bi = next_piece("glu%d" % i)
            v = wview(bi, 8, 512)
            for cc in range(2):
                c = i * 2 + cc
                for (s0, W, segs) in ((0, 512, [(0, 512, 15)]), (512, 30, [(0, 15, 0), (15, 30, 527)])):
                    ba = nb()
                    mm_group(ba.ap[:, 0:W], ba.buf, [(v[:, k, cc * 128:(cc + 1) * 128], flat(hT, k)[:, s0:s0 + W], [wbb[bi]] + colbufs(hTb, k, s0, s0 + W)) for k in range(8)])
                    bg = nb()
                    mm_group(bg.ap[:, 0:W], bg.buf, [(v[:, k, 256 + cc * 128:256 + (cc + 1) * 128], flat(hT, k)[:, s0:s0 + W], [wbb[bi]] + colbufs(hTb, k, s0, s0 + W)) for k in range(8)])
                    glu(ba, bg, W, [(zs_v[:, c, z0:z0 + (a1 - a0)], a0, a1, [zbuf_b[c]], None) for (a0, a1, z0) in segs])
                TT(zs_v[:, c, :], zs_v[:, c, :], valid_t[:, :], ALU.mult, [zbuf_b[c], constb2], [zbuf_b[c]])
        bi = next_piece("q0")
        v = wview(bi, 8, 512)
        pend = None
        ectab = (ropeEc_t[:, :], CB)
        estab = (ropeEs_t[:, :], CB)
        for c in range(4):
            bk = nb()
            mm_group(bk.ap[:, :], bk.buf, [(v[:, k, c * 128:(c + 1) * 128], hT[:, k, 0, :], [wbb[bi], hTb[k][0]]) for k in range(8)])
            fin = rope_stage(bk, 512, ectab, estab, q0_v[:, c, 0:512], [qb[c]])
            if pend is not None:
                pend()
            pend = fin
        pend()
        if mid is not None:
            for _ in mid(None):
                pass
        s_banks = Ring([bank(i) for i in (0, 1, 2)])
        for h in range(4):
            c = h
            dg = conv_build_diag(c) if c == 0 else DG()
            cbk = bank(3)
            conv_j = [0]

            def conv_some(n, c=c, dg=dg, cbk=cbk, conv_j=conv_j):
                for _ in range(n):
                    j = conv_j[0]
                    if j >= 31:
                        return
                    MM(cbk.ap[:, :], dg.ap[:, j, :], zs_v[:, c, j:j + 512], j == 0, j == 30, [dg.buf(j), zbuf_b[c]], [cbk.buf], inc=True)
                    if c < 3 and j in LAST_TAP:
                        conv_build_group(c + 1, LAST_TAP[j])
                    conv_j[0] += 1

            bo = [bank(4), bank(5)]
            bd = [bank(6), bank(7)]
            for j in range(2):
                pr = slice(j * 64, (j + 1) * 64)
                pend_p = []
                for t in range(12):
                    if t < 10:
                        bs = s_banks()
                        kbuf = kb[h] if t < 8 else kctxb
                        MM(bs.ap[:, :], k0_v[pr, h, t * 128:(t + 1) * 128], q0_v[pr, h, 0:512], True, True, [kbuf, qb[h]], [bs.buf], inc=True)
                        p = p6_ring()
                        ACT(p.ap[:, 0:512], bs.ap[:, :], AF.Exp, [bs.buf], [p.buf], scale=0.125)
                        pend_p.append((t, p))
                        conv_some(2)
                    if t >= 2:
                        tp, pp = pend_p.pop(0)
                        MM(bo[j].ap[:, :], v0_v[:, tp, h * 128:(h + 1) * 128], pp.ap[:, 0:512], tp == 0, tp == 9, [vb[tp], pp.buf], [bo[j].buf], inc=True)
                        MM(bd[j].ap[:, :], ones_t[:, :], pp.ap[:, 0:512], tp == 0, tp == 9, [constb2, pp.buf], [bd[j].buf], inc=True)
            conv_some(31)
            ACT(M[:, c, 0, :], cbk.ap[:, :], AF.Identity, [cbk.buf] + CB, [Mb[c][0]], bias=convb_t[:, c:c + 1])
            on = []
            for j in range(2):
                r = tmp_ring()
                recip_act(r, bd[j].ap[:, :], [bd[j].buf], 512)
                o = tmp_ring()
                TT(o.ap[:, :], bo[j].ap[:, :], r.ap[:, :], ALU.mult, [bo[j].buf, r.buf], [o.buf])
                on.append(o)
            STT(M[:, 4 + h, 0, :], on[1].ap[:, :], neglam, on[0].ap[:, :], ALU.mult, ALU.add, [on[0].buf, on[1].buf, lamb], [Mb[4 + h][0]])
        set_rr(range(8))
        subln_out(0, 512)
        conv_ln_silu(0, 512)
        es_ = proj_to_M("out%d_%d", l, 4, hT, hTb, [(0, 0, 512)])
        postnorm(l, ci, 2, 0, 0, 512, 0, EPS, stat=es_.bank[0])
        prenorm(l, ci, 4, 3, xT, xTb, 0, 512, 0)
        handoff(mixbufs, allhid)
        ffn(l, ci, [(0, 512)], 0)

    def sample_layer1(mid=None):
        ci = 1
        l = 1
        prenorm(l, ci, 1, 0, xT, xTb, 0, 512, 0)
        handoff(allhid, mixbufs)
        for kv in range(4):
            S.dma("pool", k1_v[:, kv, 512:768], ck1T_d[kv * 128:(kv + 1) * 128, :], writes=[kctxb])
        for t in range(2):
            for dup in range(2):
                S.dma("pool", v1_v[:, 4 + t, :, dup, :], cv1_d[t * 128:(t + 1) * 128, :].rearrange("p (k e) -> p k e", k=4), writes=[vb[4 + t]])
        ckpt("s1_a")
        ectq = (ropeEc_t[:, 128:384], CB)
        estq = (ropeEs_t[:, 128:384], CB)
        ectab = (ropeEc_t[:, :], CB)
        estab = (ropeEs_t[:, :], CB)
        pend = None
        for i in range(2):
            bi = next_piece("q1_%d" % i)
            v = wview(bi, 8, 512)
            for cc in range(4):
                c = i * 4 + cc
                bk = nb()
                mm_group(bk.ap[:, 0:256], bk.buf, [(v[:, k, cc * 128:(cc + 1) * 128], hT[:, k, 0, 128:384], [wbb[bi], hTb[k][0]]) for k in range(8)])
                fin = rope_stage(bk, 256, ectq, estq, q1_v[:, c, 0:256], [qb[c]])
                if pend is not None:
                    pend()
                pend = fin
        bi = next_piece("k1")
        v = wview(bi, 8, 512)
        for kv in range(4):
            bk = nb()
            mm_group(bk.ap[:, :], bk.buf, [(v[:, k, kv * 128:(kv + 1) * 128], hT[:, k, 0, :], [wbb[bi], hTb[k][0]]) for k in range(8)])
            fin = rope_stage(bk, 512, ectab, estab, k1_v[:, kv, 0:512], [kb[kv]])
            pend()
            pend = fin
        pend()
        bi = next_piece("v1")
        v = wview(bi, 8, 256)
        for tl in range(4):
            bk = nb()
            mm_group(bk.ap[:, 0:256], bk.buf, [(hT[:, k, 0, tl * 128:(tl + 1) * 128], v[:, k, :], [wbb[bi], hTb[k][0]]) for k in range(8)])
            for dup in range(2):
                ACT(v1_v[:, tl, :, dup, :], bk.ap[:, 0:256].rearrange("p (k e) -> p k e", k=4), AF.Copy, [bk.buf], [vb[tl]])
        if mid is not None:
            for _ in mid(None):
                pass
        s_banks = Ring([bank(i) for i in (0, 1, 2, 3)])
        ma = mask1_t[:, :]
        pend_p = []

        def att_pv(c, kt, pp):
            kv = c // 2
            bo, bd = bank(4 + c % 2), bank(6 + c % 2)
            MM(bo.ap[:, :], v1_v[:, kt, kv, :, :].rearrange("p d e -> p (d e)"), pp.ap[:, 0:512],
               kt == 0, kt == 5, [vb[kt], pp.buf], [bo.buf], inc=True)
            MM(bd.ap[:, :], ones_t[:, :], pp.ap[:, 0:512], kt == 0, kt == 5, [constb2, pp.buf], [bd.buf], inc=True)
            if kt == 5:
                r = tmp_ring()
                for h2 in range(2):
                    pr = slice(h2 * 64, (h2 + 1) * 64)
                    ACT(r.ap[pr, 0:256], bd.ap[pr, h2 * 256:(h2 + 1) * 256], AF.Ln, [bd.buf, sinkb], [r.buf], bias=sink_t[pr, c:c + 1])
                ACT(r.ap[:, 0:256], r.ap[:, 0:256], AF.Exp, [r.buf], [r.buf], scale=-1.0)
                for h2 in range(2):
                    pr = slice(h2 * 64, (h2 + 1) * 64)
                    TT(hT[pr, c, 0, 0:256], bo.ap[pr, h2 * 256:(h2 + 1) * 256], r.ap[pr, 0:256], ALU.mult, [bo.buf, r.buf], [hTb[c][0]])

        for c in range(8):
            kv = c // 2
            for kt in range(6):
                kbuf = kb[kv] if kt < 4 else kctxb
                pp = p6_ring()
                for hp in range(2):
                    pr = slice(hp * 64, (hp + 1) * 64)
                    bs = s_banks()
                    MM(bs.ap[:, 0:256], k1_v[pr, kv, kt * 128:(kt + 1) * 128], q1_v[pr, c, 0:256],
                       True, True, [kbuf, qb[c]], [bs.buf], inc=True)
                    ACT(pp.ap[:, hp * 256:(hp + 1) * 256], bs.ap[:, 0:256], AF.Exp, [bs.buf], [pp.buf], scale=0.125)
                if kt < 4:
                    mk = bass.AP(ma.tensor, ma.offset + kt * 256, [list(ma.ap[0]), [0, 2], [1, 256]])
                    TT(pp.ap[:, 0:512].rearrange("p (h q) -> p h q", h=2), pp.ap[:, 0:512].rearrange("p (h q) -> p h q", h=2),
                       mk, ALU.mult, [pp.buf, constb2], [pp.buf])
                pend_p.append((c, kt, pp))
                if len(pend_p) > 2:
                    att_pv(*pend_p.pop(0))
        while pend_p:
            att_pv(*pend_p.pop(0))
        set_rr(range(8))
        ckpt("s1_attn")
        es_ = proj_to_M("out%d_%d", l, 4, hT, hTb, [(0, 0, 256)])
        postnorm(l, ci, 2, 0, 0, 256, 128, EPS, stat=es_.bank[0])
        ckpt("s1_out")
        prenorm(l, ci, 4, 3, xT, xTb, 128, 256, 0)
        handoff(mixbufs, allhid)
        ffn(l, ci, [(0, 256)], 128)

    try:
        if sample_first:
            do_mod(0, 0)
            sample_layer0(mid=lambda bk: mod_steps(0, 1, bk))
            ckpt("f_s0")
            do_mod(1, 0)
            sample_layer1(mid=lambda bk: mod_steps(1, 1, bk))
            store_y([(0, 128, 256, 1024)])
            ckpt("f_s1")
            raise StopBuild("end")
        ckpt("setup")
        for g in range(2):
            S.dma("sp", xT[:, :, g, :], xp_d[:, g * 512:(g + 1) * 512].rearrange("(c p) t -> p c t", p=128),
                  writes=[xTb[c][g] for c in range(8)])
        do_mod(0, 0)
        ckpt("mod0")
        prompt_layer0(mid=lambda bk: mod_steps(0, 1, bk), fuse_next=True)
        ckpt("p0")
        prompt_layer1(mid=lambda bk: mod_steps(1, 1, bk), skip_prenorm=True)
        ckpt("p1")
        load_xsa(0)
        store_y([(0, 0, 512, 0)])
        load_xsa(1)
        store_y([(1, 0, 512, 512)])
        xsa_loaded.append(True)
        sample_layer0()
        ckpt("s0")
        sample_layer1()
        store_y([(0, 128, 256, 1024)])
    except StopBuild:
        pass

    fin_waits = list(S.final.items())
    sp = S.eng["sp"]
    sp.ops.append((fin_waits, None, 0, None))

    with nc.Block() as block:
        @block.tensor
        def _(e):
            S.replay("pe", e)

        @block.scalar
        def _(e):
            S.replay("act", e)

        @block.vector
        def _(e):
            S.replay("dve", e)

        @block.gpsimd
        def _(e):
            S.replay("pool", e)

        @block.sync
        def _(e):
            eng = S.eng["sp"]
            for waits, fn, inc, sem in eng.ops:
                for s, v in waits:
                    e.wait_ge(s.h, v)
                if fn is None:
                    continue
                ins = fn(e)
                if inc:
                    ins.then_inc(sem.h, inc)
    stack.close()
    return nc


def _rope_tables(pos):
    p = np.arange(128)
    q = p % 64
    half = q // 32
    idx = q % 32
    part = idx // 16
    i = idx % 16
    inv = (1.0 / (ROPE_THETA ** (np.arange(0, 32, 2, dtype=np.float32) / np.float32(32)))).astype(np.float32)
    rows = np.floor_divide(pos, 64).astype(np.float32)
    cols = np.mod(pos, 64).astype(np.float32)
    posv = np.where(half[:, None] == 0, rows[None, :], cols[None, :]).astype(np.float32)
    ang = (posv * inv[i][:, None]).astype(np.float32)
    c = np.cos(ang).astype(np.float32)
    s = np.sin(ang).astype(np.float32)
    s = np.where(part[:, None] == 0, -s, s).astype(np.float32)
    return np.ascontiguousarray(c), np.ascontiguousarray(s)


def _pp(vec, nch):
    return np.ascontiguousarray(np.asarray(vec, np.float32).reshape(nch, 128).T)


_NC_CACHE = {}


def _prep(x_prompt, x_sample, cache_k0, cache_v0, cache_k1, cache_v1, c, c_ctx,
          l0_mod_w, l0_mod_b, l0_norm_g, l0_w_in, l0_conv_w, l0_conv_b, l0_conv_ln_g,
          l0_conv_ln_b, l0_lambda, l0_subln_g, l0_w_out, l0_w_gu, l0_w_down,
          l1_mod_w, l1_mod_b, l1_norm_g, l1_w_qkv, l1_sink, l1_w_out, l1_w_gu, l1_w_down):
    f = lambda a: np.ascontiguousarray(np.asarray(a, dtype=np.float32))
    x_prompt, x_sample = f(x_prompt), f(x_sample)
    cache_k0, cache_v0, cache_k1, cache_v1 = f(cache_k0), f(cache_v0), f(cache_k1), f(cache_v1)
    c, c_ctx = f(c), f(c_ctx)

    shared = {
        "modw0": f(l0_mod_w), "modw1": f(l1_mod_w),
        "gn0": np.ascontiguousarray(f(l0_norm_g).reshape(4, 8, 128).transpose(2, 0, 1).reshape(128, 32)),
        "gn1": np.ascontiguousarray(f(l1_norm_g).reshape(4, 8, 128).transpose(2, 0, 1).reshape(128, 32)),
        "w_in": f(l0_w_in), "w_qkv": f(l1_w_qkv),
        "w_out0": f(l0_w_out), "w_out1": f(l1_w_out),
        "w_gu0": f(l0_w_gu), "w_gu1": f(l1_w_gu),
        "w_down0": f(l0_w_down), "w_down1": f(l1_w_down),
        "convw": np.ascontiguousarray(f(l0_conv_w).reshape(31, 4, 128).transpose(2, 1, 0).reshape(128, 124)),
        "convb": _pp(l0_conv_b, 4), "lng": _pp(l0_conv_ln_g, 4), "lnb": _pp(l0_conv_ln_b, 4),
        "lam": np.ascontiguousarray(np.broadcast_to(f(l0_lambda).reshape(1, 256), (128, 256))),
        "subg": np.ascontiguousarray(f(l0_subln_g).reshape(128, 1)),
        "sink": np.ascontiguousarray(np.repeat(f(l1_sink).reshape(8, 2), 64, axis=1).T),
        "ident": np.eye(128, dtype=np.float32),
    }
    pidx = np.arange(128)
    partner = np.where((pidx % 32) < 16, pidx + 16, pidx - 16)
    perm = np.zeros((128, 128), np.float32)
    perm[partner, pidx] = 1.0
    shared["perm"] = perm
    for l, mb in ((0, l0_mod_b), (1, l1_mod_b)):
        shared["modb%d" % l] = np.ascontiguousarray(np.repeat(_pp(mb, 48), 2, axis=1))
    ropeAc, ropeAs = _rope_tables(np.arange(1024))
    shared["ropeAc"], shared["ropeAs"] = ropeAc, ropeAs

    in_maps = []
    for i in range(NCORES):
        b, qt = i // 4, i % 4
        base = 256 * qt - 128
        m = dict(shared)
        m["xp"] = np.ascontiguousarray(x_prompt[4 * i:4 * i + 4].reshape(1024, D).T)
        xs = x_sample[b]
        m["xsa"] = np.ascontiguousarray(xs.T)
        padded = np.zeros((1024 + 2 * 143, D), np.float32)
        padded[143:143 + 1024] = xs
        pe0 = 143 + base
        ext = np.concatenate([padded[pe0:pe0 + 512], padded[pe0 - 15:pe0], padded[pe0 + 512:pe0 + 527]], axis=0)
        m["xse"] = np.ascontiguousarray(ext.T)
        m["ck0T"] = np.ascontiguousarray(cache_k0[b].reshape(256, 512).T)
        m["cv0"] = np.ascontiguousarray(cache_v0[b].reshape(256, 512))
        k1t = cache_k1[b].reshape(256, 4, 64).transpose(1, 2, 0)
        m["ck1T"] = np.ascontiguousarray(np.concatenate([k1t, k1t], axis=1).reshape(512, 256))
        m["cv1"] = np.ascontiguousarray(cache_v1[b].reshape(256, 256))
        ct = np.stack([c_ctx.reshape(8, 128), c[b].reshape(8, 128)], axis=-1)
        m["condT"] = np.ascontiguousarray(ct.transpose(1, 0, 2).reshape(128, 16))
        epos = base + np.arange(512)
        m["ropeEc"], m["ropeEs"] = _rope_tables(epos)
        zpos = base - 15 + np.arange(542)
        vz = ((zpos >= 0) & (zpos < 1024)).astype(np.float32)
        m["valid"] = np.ascontiguousarray(np.broadcast_to(vz[None, :], (128, 542)))
        mk = np.zeros((128, 1024), np.float32)
        kk = np.arange(128)[:, None]
        ii = np.arange(128)[None, :]
        for k in range(4):
            kpos = base + 128 * k + kk
            for qi, q in enumerate((1, 2)):
                qpos = base + 128 * q + ii
                ok = (np.abs(qpos - kpos) <= 128) & (kpos >= 0) & (kpos < 1024)
                mk[:, 256 * k + 128 * qi:256 * k + 128 * (qi + 1)] = ok
        m["mask1"] = mk
        in_maps.append(m)

    return in_maps


def _assemble(R):
    y_prompt = np.empty((32, 256, D), np.float32)
    y_sample = np.empty((2, 1024, D), np.float32)
    new_k0 = np.empty((32, 256, 4, 2, 64), np.float32)
    new_v0 = np.empty((32, 256, 4, 128), np.float32)
    new_k1 = np.empty((32, 256, 4, 64), np.float32)
    new_v1 = np.empty((32, 256, 4, 64), np.float32)
    for i in range(NCORES):
        b, qt = i // 4, i % 4
        r = R[i]
        yT = np.asarray(r["yT"])
        y_prompt[4 * i:4 * i + 4] = yT[:, 0:1024].T.reshape(4, 256, D)
        y_sample[b, 256 * qt:256 * qt + 256] = yT[:, 1024:1280].T
        new_k0[4 * i:4 * i + 4] = np.asarray(r["k0T"]).T.reshape(4, 256, 4, 2, 64)
        new_v0[4 * i:4 * i + 4] = np.asarray(r["v0"]).reshape(4, 256, 4, 128)
        new_k1[4 * i:4 * i + 4] = np.asarray(r["k1T"]).T.reshape(4, 256, 4, 64)
        new_v1[4 * i:4 * i + 4] = np.asarray(r["v1"]).reshape(4, 256, 4, 64)
    return (y_prompt, y_sample, new_k0, new_v0, new_k1, new_v1)


def kernel(**inputs):
    if "nc" not in _NC_CACHE:
        _NC_CACHE["nc"] = build_program()
    nc = _NC_CACHE["nc"]
    in_maps = _prep(**inputs)
    res = run_bass_kernel_spmd(nc, in_maps, core_ids=list(range(NCORES)))
    return _assemble(res.results)
```

```python
import math
from contextlib import ExitStack

import numpy as np
import concourse.bass as bass
import concourse.mybir as mybir
from concourse.bass_utils import run_bass_kernel_spmd

F32 = mybir.dt.float32
BF16 = mybir.dt.bfloat16
AF = mybir.ActivationFunctionType
ALU = mybir.AluOpType
AX = mybir.AxisListType

D = 1024
KC = 8
FF = 2816
JC = 22
EPS = 1e-6
NCORES = 8
NWB = 4
WBN = 4096
ROPE_THETA = 10000.0


class Sem:
    def __init__(self, h, name):
        self.h = h
        self.name = name


class Buf:
    __slots__ = ("name", "w", "rs", "dsem", "dcnt", "excl")

    def __init__(self, name, excl=False):
        self.name = name
        self.excl = excl
        self.w = None
        self.rs = {}
        self.dsem = None
        self.dcnt = 0


class Eng:
    def __init__(self, name, sem):
        self.name = name
        self.sem = sem
        self.count = 0
        self.known = {}
        self.ops = []


class Sched:
    def __init__(self, nc, stack):
        self.nc = nc
        self.stack = stack
        self.eng = {}
        for n in ("pe", "act", "dve", "pool", "sp"):
            self.eng[n] = Eng(n, self.new_sem("e_" + n))
        self.final = {}

    def new_sem(self, name):
        h = self.stack.enter_context(self.nc.semaphore(name))
        return Sem(h, name)

    def _needs(self, eng, reads, writes):
        need = {}

        def req(ev):
            if ev is None:
                return
            sem, val = ev
            if sem is eng.sem and eng.name == "pe":
                return
            if eng.known.get(sem, 0) >= val:
                return
            if need.get(sem, 0) < val:
                need[sem] = val

        for b in reads:
            req(b.w)
            if b.excl:
                for s, v in b.rs.items():
                    if s is not eng.sem:
                        req((s, v))
        for b in writes:
            req(b.w)
            for s, v in b.rs.items():
                req((s, v))
        for s, v in need.items():
            eng.known[s] = v
        return list(need.items())

    def op(self, en, fn, reads=(), writes=(), inc=True):
        eng = self.eng[en]
        waits = self._needs(eng, reads, writes)
        if inc:
            eng.count += 1
            ev = (eng.sem, eng.count)
        else:
            ev = (eng.sem, eng.count + 1)
        eng.ops.append((waits, fn, 1 if inc else 0, eng.sem))
        for b in reads:
            if b.rs.get(ev[0], 0) < ev[1]:
                b.rs[ev[0]] = ev[1]
        for b in writes:
            b.w = ev
            b.rs = {}

    def dma(self, qn, out_ap, in_ap, reads=(), writes=(), final=False, track=None):
        eng = self.eng[qn]
        tb = track if track is not None else (list(writes) + list(reads))[0]
        wr = [b for b in writes if not (b.w is not None and b.w[0] is tb.dsem and not b.rs)]
        waits = self._needs(eng, reads, wr)
        if tb.dsem is None:
            tb.dsem = self.new_sem("d_" + tb.name)
        tb.dcnt += 16
        ev = (tb.dsem, tb.dcnt)
        eng.ops.append((waits, (lambda e, o=out_ap, i=in_ap: e.dma_start(out=o, in_=i)), 16, tb.dsem))
        for b in reads:
            if b.rs.get(ev[0], 0) < ev[1]:
                b.rs[ev[0]] = ev[1]
        for b in writes:
            b.w = ev
            b.rs = {}
        if final:
            if self.final.get(ev[0], 0) < ev[1]:
                self.final[ev[0]] = ev[1]
        return ev

    def replay(self, en, e):
        eng = self.eng[en]
        for waits, fn, inc, sem in eng.ops:
            for s, v in waits:
                e.wait_ge(s.h, v)
            ins = fn(e)
            if inc:
                ins.then_inc(sem.h, inc)


def handoff(old_bufs, new_bufs):
    merged = {}
    for b in old_bufs:
        if b.w is not None:
            s, v = b.w
            if merged.get(s, 0) < v:
                merged[s] = v
        for s, v in b.rs.items():
            if merged.get(s, 0) < v:
                merged[s] = v
    for b in new_bufs:
        b.w = None
        b.rs = dict(merged)


class Ring:
    def __init__(self, items):
        self.items = items
        self.i = 0

    def __call__(self):
        it = self.items[self.i % len(self.items)]
        self.i += 1
        return it


class TB:
    __slots__ = ("ap", "buf")

    def __init__(self, ap, buf):
        self.ap = ap
        self.buf = buf


class StopBuild(Exception):
    pass


def build_program(stop_at=None):
    import os
    stop_at = stop_at or os.environ.get("MK_STOP_AT")

    def ckpt(name):
        if stop_at is not None and name == stop_at:
            raise StopBuild(name)

    nc = bass.Bass("TRN2", target_bir_lowering=False)
    stack = ExitStack()
    S = Sched(nc, stack)

    def din(name, shape):
        return nc.dram_tensor(name, list(shape), F32, kind="ExternalInput").ap()

    def dout(name, shape):
        return nc.dram_tensor(name, list(shape), F32, kind="ExternalOutput").ap()

    xp_d = din("xp", [D, 1024])
    xsa_d = din("xsa", [D, 1024])
    xse_d = din("xse", [D, 542])
    ck0T_d = din("ck0T", [512, 256])
    cv0_d = din("cv0", [256, 512])
    ck1T_d = din("ck1T", [512, 256])
    cv1_d = din("cv1", [256, 256])
    condT_d = din("condT", [128, 16])
    modw_d = [din("modw0", [D, 6 * D]), din("modw1", [D, 6 * D])]
    modb_d = [din("modb0", [128, 96]), din("modb1", [128, 96])]
    gn_d = [din("gn0", [128, 32]), din("gn1", [128, 32])]
    win_d = din("w_in", [D, 2560])
    wqkv_d = din("w_qkv", [D, 1536])
    wout_d = [din("w_out0", [D, D]), din("w_out1", [D, D])]
    wgu_d = [din("w_gu0", [D, 2 * FF]), din("w_gu1", [D, 2 * FF])]
    wdn_d = [din("w_down0", [FF, D]), din("w_down1", [FF, D])]
    convw_d = din("convw", [128, 124])
    convb_d = din("convb", [128, 4])
    lng_d = din("lng", [128, 4])
    lnb_d = din("lnb", [128, 4])
    lam_d = din("lam", [128, 256])
    subg_d = din("subg", [128, 1])
    sink_d = din("sink", [128, 8])
    ropeAc_d = din("ropeAc", [128, 1024])
    ropeAs_d = din("ropeAs", [128, 1024])
    ropeEc_d = din("ropeEc", [128, 512])
    ropeEs_d = din("ropeEs", [128, 512])
    valid_d = din("valid", [128, 542])
    mask1_d = din("mask1", [128, 1024])
    ident_d = din("ident", [128, 128])
    perm_d = din("perm", [128, 128])

    yT_d = dout("yT", [D, 1280])
    k0T_d = dout("k0T", [512, 1024])
    v0_d = dout("v0", [1024, 512])
    k1T_d = dout("k1T", [256, 1024])
    v1_d = dout("v1", [1024, 256])

    def sb(name, shape, dt):
        return stack.enter_context(nc.sbuf_tensor("s_" + name, list(shape), dt))

    xT = sb("xT", [128, 8, 2, 512], F32)
    hT = sb("hT", [128, 8, 2, 512], BF16)
    M = sb("M", [128, 8, 2, 512], F32)
    big = sb("big", [128, JC * 1024], BF16)
    wb = sb("wb", [128, NWB, WBN], BF16)
    sqr = sb("sqr", [128, 3, 512], BF16)
    rstd_t = sb("rstd", [128, 2, 512], F32)
    tmp_t = sb("tmp", [128, 5, 512], F32)
    stg_t = sb("stg", [128, 3, 512], F32)
    xb_t = sb("xb", [128, 2, 512], BF16)
    pbig = sb("pbig", [128, 2, 1536], BF16)
    diag_t = sb("diag", [128, 1, 31, 128], BF16)
    mean_t = sb("mean", [128, 1, 512], F32)
    ones_t = sb("ones", [128, 128], BF16)
    ident_t = sb("identb", [128, 128], BF16)
    perm_t = sb("permb", [128, 128], BF16)
    condT_t = sb("condT", [128, 16], F32)
    sc_t = sb("sc", [128, 16], BF16)
    mpar = [sb("mpar0", [128, 96], F32), sb("mpar1", [128, 96], F32)]
    modb_t = [sb("modbt0", [128, 96], F32), sb("modbt1", [128, 96], F32)]
    gn_t = [sb("gnt0", [128, 32], F32), sb("gnt1", [128, 32], F32)]
    convw_t = sb("convwt", [128, 124], F32)
    convb_t = sb("convbt", [128, 4], F32)
    lng_t = sb("lngt", [128, 4], F32)
    lnb_t = sb("lnbt", [128, 4], F32)
    lam_t = mean_t[:, 0, 0:256]
    lamw = sb("lamw", [128, 8], F32)
    subg_t = sb("subgt", [128, 1], F32)
    sink_t = sb("sinkt", [128, 8], F32)
    ropeEc_t = sb("ropeEc", [128, 512], F32)
    ropeEs_t = sb("ropeEs", [128, 512], F32)
    valid_t = sb("validt", [128, 542], BF16)
    mask1_t = sb("mask1t", [128, 1024], BF16)

    banks = [stack.enter_context(nc.psum_tensor("ps%d" % i, [128, 512], F32)) for i in range(8)]
    bankb = [Buf("bank%d" % i, excl=True) for i in range(8)]

    xTb = [[Buf("xT%d_%d" % (c, g)) for g in range(2)] for c in range(8)]
    hTb = [[Buf("hT%d_%d" % (c, g)) for g in range(2)] for c in range(8)]
    Mb = [[Buf("M%d_%d" % (c, g)) for g in range(2)] for c in range(8)]
    hidb = [[Buf("hid%d_%d" % (j, g)) for g in range(2)] for j in range(JC)]
    wbb = [Buf("wb%d" % i) for i in range(NWB)]
    constb = Buf("const")

    sq_ring = Ring([TB(sqr[:, i, :], Buf("sq%d" % i)) for i in range(3)])
    rstd_ring = Ring([TB(rstd_t[:, i, :], Buf("rstd%d" % i)) for i in range(2)])
    tmp_ring = Ring([TB(tmp_t[:, i, :], Buf("tmp%d" % i)) for i in range(5)])
    stg_ring = Ring([TB(stg_t[:, i, :], Buf("stg%d" % i)) for i in range(3)])
    xb_ring = Ring([TB(xb_t[:, i, :], Buf("xb%d" % i)) for i in range(2)])
    pflat = pbig[:, :, :].rearrange("p a b -> p (a b)")
    p6_ring = Ring([TB(pflat[:, i * 512:(i + 1) * 512], Buf("p6_%d" % i)) for i in range(6)])
    pair_ring = Ring([TB(pflat[:, i * 1024:(i + 1) * 1024], Buf("pp_%d" % i)) for i in range(3)])
    diag_ring = Ring([TB(diag_t[:, i, :, :], Buf("diag%d" % i)) for i in range(1)])
    mean_ring = Ring([TB(mean_t[:, i, :], Buf("mean%d" % i)) for i in range(1)])
    rr_state = {"banks": list(range(8)), "i": 0}

    def nb():
        lst = rr_state["banks"]
        i = lst[rr_state["i"] % len(lst)]
        rr_state["i"] += 1
        return TB(banks[i], bankb[i])

    def bank(i):
        return TB(banks[i], bankb[i])

    def set_rr(lst):
        rr_state["banks"] = list(lst)
        rr_state["i"] = 0

    def ACT(out, in_, func, reads, writes, scale=1.0, bias=0.0):
        S.op("act", lambda e, o=out, i=in_, f=func, s=scale, b=bias: e.activation(out=o, in_=i, func=f, bias=b, scale=s),
             reads, writes)

    def TT(out, a, b, op, reads, writes):
        S.op("dve", lambda e, o=out, a=a, b=b, op=op: e.tensor_tensor(o, a, b, op), reads, writes)

    def TS(out, a, s1, s2, op0, op1, reads, writes):
        if s2 is None:
            S.op("dve", lambda e, o=out, a=a, s1=s1, op0=op0: e.tensor_scalar(o, a, s1, None, op0), reads, writes)
        else:
            S.op("dve", lambda e, o=out, a=a, s1=s1, s2=s2, op0=op0, op1=op1: e.tensor_scalar(o, a, s1, s2, op0, op1),
                 reads, writes)

    def STT(out, a, scalar, b, op0, op1, reads, writes):
        S.op("dve", lambda e, o=out, a=a, sc=scalar, b=b, op0=op0, op1=op1: e.scalar_tensor_tensor(o, a, sc, b, op0, op1),
             reads, writes)

    def RECIP(out, a, reads, writes):
        S.op("dve", lambda e, o=out, a=a: e.reciprocal(o, a), reads, writes)

    def COPYV(out, a, reads, writes):
        S.op("dve", lambda e, o=out, a=a: e.tensor_copy(o, a), reads, writes)

    def MM(out, lhsT, rhs, start, stop, reads, writes, inc):
        S.op("pe", lambda e, o=out, l=lhsT, r=rhs, st=start, sp=stop: e.matmul(o, l, r, start=st, stop=sp),
             reads, writes, inc=True)

    def mm_group(out_ap, out_buf, items):
        n = len(items)
        for i, (l, r, rd) in enumerate(items):
            MM(out_ap, l, r, i == 0, i == n - 1, rd, [out_buf], inc=(i == n - 1))

    cev = []

    def cload(q, dst_ap, src_ap):
        cev.append(S.dma(q, dst_ap, src_ap, writes=[constb]))

    cload("sp", condT_t[:, :], condT_d[:, :])
    for l in range(2):
        cload("sp", modb_t[l][:, :], modb_d[l][:, :])
        cload("sp", gn_t[l][:, :], gn_d[l][:, :])
    cload("sp", convw_t[:, :], convw_d[:, :])
    cload("sp", convb_t[:, :], convb_d[:, :])
    cload("sp", lng_t[:, :], lng_d[:, :])
    cload("sp", lnb_t[:, :], lnb_d[:, :])
    cload("sp", lam_t, lam_d[:, :])
    cload("sp", subg_t[:, :], subg_d[:, :])
    cload("sp", sink_t[:, :], sink_d[:, :])
    cload("sp", ropeEc_t[:, :], ropeEc_d[:, :])
    cload("sp", ropeEs_t[:, :], ropeEs_d[:, :])
    constb2 = Buf("const2")
    S.dma("pool", ident_t[:, :], ident_d[:, :], writes=[constb2])
    S.dma("pool", perm_t[:, :], perm_d[:, :], writes=[constb2])
    S.dma("pool", valid_t[:, :], valid_d[:, :], writes=[constb2])
    S.dma("pool", mask1_t[:, :], mask1_d[:, :], writes=[constb2])
    CB = [constb, constb2]

    S.op("dve", lambda e: e.memset(ones_t[:, :], 1.0), [], [constb2])

    plan = []
    wstate = {"issued": 0, "pos": 0}

    def wview(bi, a, b):
        return wb[:, bi, 0:a * b].rearrange("p (a b) -> p a b", b=b)

    def add_piece(tag, parts):
        plan.append((tag, parts))

    def issue_piece(idx):
        tag, parts = plan[idx]
        bi = idx % NWB
        for (kch, tot, off, n, src) in parts:
            v = wview(bi, kch, tot)
            S.dma("pool", v[:, :, off:off + n], src, writes=[wbb[bi]])

    def next_piece(tag, hold=0):
        pos = wstate["pos"]
        while wstate["issued"] < min(len(plan), pos + NWB - hold):
            issue_piece(wstate["issued"])
            wstate["issued"] += 1
        ptag, parts = plan[pos]
        assert ptag == tag, (ptag, tag)
        wstate["pos"] += 1
        return pos % NWB

    def wsrc(w_ap, c0, n):
        return w_ap[:, c0:c0 + n].rearrange("(c p) n -> p c n", p=128)

    def plan_mod_piece(l, i):
        add_piece("mod%d_%d" % (l, i), [(8, 512, 0, 512, wsrc(modw_d[l], i * 512, 512))])

    def plan_mod(l, part):
        p0, p1 = (0, 4) if part == 0 else (4, 12)
        for i in range(p0, p1):
            add_piece("mod%d_%d" % (l, i), [(8, 512, 0, 512, wsrc(modw_d[l], i * 512, 512))])

    def plan_layer(kind, l, with_mod=False, first_mod=True, next_mod=None):
        if with_mod and first_mod:
            plan_mod(l, 0)
        plan_layer_in(kind, l)
        if with_mod:
            plan_mod(l, 1)
        plan_layer_rest(kind, l, next_mod)

    def plan_layer_in(kind, l):
        if l == 0:
            if kind == "sample":
                add_piece("k0", [(8, 512, 0, 512, wsrc(win_d, 1536, 512))])
                add_piece("v0", [(8, 512, 0, 512, wsrc(win_d, 2048, 512))])
            for i in range(2):
                add_piece("glu%d" % i, [(8, 512, 0, 256, wsrc(win_d, i * 256, 256)),
                                        (8, 512, 256, 256, wsrc(win_d, 512 + i * 256, 256))])
            add_piece("q0", [(8, 512, 0, 512, wsrc(win_d, 1024, 512))])
            if kind == "prompt":
                add_piece("k0", [(8, 512, 0, 512, wsrc(win_d, 1536, 512))])
                add_piece("v0", [(8, 512, 0, 512, wsrc(win_d, 2048, 512))])
        else:
            for i in range(2):
                add_piece("q1_%d" % i, [(8, 512, 0, 512, wsrc(wqkv_d, i * 512, 512))])
            parts = []
            for kv in range(4):
                for dup in range(2):
                    parts.append((8, 512, kv * 128 + dup * 64, 64, wsrc(wqkv_d, 1024 + kv * 64, 64)))
            add_piece("k1", parts)
            add_piece("v1", [(8, 256, 0, 256, wsrc(wqkv_d, 1280, 256))])

    def plan_layer_rest(kind, l, next_mod=None):
        for i in range(2):
            add_piece("out%d_%d" % (l, i), [(8, 512, 0, 512, wsrc(wout_d[l], i * 512, 512))])
        for i in range(11):
            if next_mod is not None and i in (3, 5, 7, 9):
                plan_mod_piece(next_mod, (3, 5, 7, 9).index(i))
            add_piece("gu%d_%d" % (l, i), [(8, 512, 0, 256, wsrc(wgu_d[l], i * 256, 256)),
                                           (8, 512, 256, 256, wsrc(wgu_d[l], FF + i * 256, 256))])
        for c in range(8):
            add_piece("dn%d_%d" % (l, c), [(JC, 128, 0, 128,
                                            wdn_d[l][:, c * 128:(c + 1) * 128].rearrange("(j p) n -> p j n", p=128))])

    sample_first = bool(os.environ.get("MK_SAMPLE_FIRST"))
    if sample_first:
        plan_layer("sample", 0, True)
        plan_layer("sample", 1, True)
        plan_layer("prompt", 0)
        plan_layer("prompt", 1)
    else:
        plan_layer("prompt", 0, True, next_mod=1)
        plan_layer("prompt", 1, True, first_mod=False)
        plan_layer("sample", 0)
        plan_layer("sample", 1)

    scb = Buf("scb")
    ACT(sc_t[:, :], condT_t[:, :], AF.Silu, CB, [scb])

    LAM_INIT0 = 0.8 - 0.6 * math.exp(0.0)
    lamb = Buf("lamb")
    t1 = tmp_ring()
    TT(t1.ap[:, 0:64], lam_t[:, 0:64], lam_t[:, 64:128], ALU.mult, CB + [mean_ring.items[0].buf], [t1.buf])
    TT(t1.ap[:, 64:128], lam_t[:, 128:192], lam_t[:, 192:256], ALU.mult, CB + [mean_ring.items[0].buf], [t1.buf])
    S.op("dve", lambda e: e.reduce_sum(lamw[:, 0:1], t1.ap[:, 0:64], AX.X), [t1.buf], [lamb])
    S.op("dve", lambda e: e.reduce_sum(lamw[:, 1:2], t1.ap[:, 64:128], AX.X), [t1.buf], [lamb])
    ACT(lamw[:, 2:4], lamw[:, 0:2], AF.Exp, [lamb], [lamb])
    TT(lamw[:, 4:5], lamw[:, 3:4], lamw[:, 2:3], ALU.subtract, [lamb], [lamb])
    TS(lamw[:, 5:6], lamw[:, 4:5], -LAM_INIT0, None, ALU.add, None, [lamb], [lamb])
    TS(lamw[:, 6:7], subg_t[:, 0:1], 1.0 - LAM_INIT0, None, ALU.mult, None, [lamb] + CB, [lamb])
    neglam = lamw[:, 5:6]
    sgs = lamw[:, 6:7]
    sinkb = Buf("sinkb")
    ACT(sink_t[:, :], sink_t[:, :], AF.Exp, CB, [sinkb])

    mparbA = [Buf("mparA0"), Buf("mparA1")]
    mparbB = [Buf("mparB0"), Buf("mparB1")]

    def parbuf(l, n):
        return mparbA[l] if n < 2 else mparbB[l]

    def do_mod(l, part, bk=None):
        for _ in mod_steps(l, part, bk):
            pass

    def mod_steps(l, part, bk=None):
        if bk is None:
            bk = nb()
        p0, p1 = (0, 4) if part == 0 else (4, 12)
        pb = mparbA[l] if part == 0 else mparbB[l]
        for i in range(p0, p1):
            bi = next_piece("mod%d_%d" % (l, i))
            v = wview(bi, 8, 512)
            for fc in range(4):
                col = (i * 4 + fc) * 2
                items = [(v[:, k, fc * 128:(fc + 1) * 128], sc_t[:, k * 2:k * 2 + 2], [wbb[bi], scb]) for k in range(8)]
                mm_group(bk.ap[:, col:col + 2], bk.buf, items)
            if i < p1 - 1:
                yield i
        mp = mpar[l]
        c0, c1 = p0 * 8, p1 * 8
        TT(mp[:, c0:c1], bk.ap[:, c0:c1], modb_t[l][:, c0:c1], ALU.add, [bk.buf] + CB, [pb])
        m3 = mp[:, :].rearrange("p (n c i) -> p n c i", n=6, c=8, i=2)
        g = gn_t[l]
        for ci in range(2):
            if part == 0:
                STT(m3[:, 1, :, ci], m3[:, 1, :, ci], 1.0, g[:, 0:8], ALU.add, ALU.mult, [pb] + CB, [pb])
            else:
                TT(m3[:, 2, :, ci], m3[:, 2, :, ci], g[:, 8:16], ALU.mult, [pb] + CB, [pb])
                STT(m3[:, 4, :, ci], m3[:, 4, :, ci], 1.0, g[:, 16:24], ALU.add, ALU.mult, [pb] + CB, [pb])
                TT(m3[:, 5, :, ci], m3[:, 5, :, ci], g[:, 24:32], ALU.mult, [pb] + CB, [pb])
        yield -1

    def par(l, n, c, ci):
        i = (n * 8 + c) * 2 + ci
        return mpar[l][:, i:i + 1]

    def flat(t, c):
        return t[:, c, :, :].rearrange("p g t -> p (g t)")

    def colbufs(bufs2d, c, c0, c1):
        gs = sorted(set([c0 // 512, (c1 - 1) // 512]))
        return [bufs2d[c][g] for g in gs]

    def rstd_from(bk_ap, bk_buf, W, n, eps):
        t = tmp_ring()
        r = rstd_ring()
        ACT(t.ap[:, 0:W], bk_ap, AF.Ln, [bk_buf], [t.buf], scale=1.0 / n, bias=eps)
        ACT(r.ap[:, 0:W], t.ap[:, 0:W], AF.Exp, [t.buf], [r.buf], scale=-0.5)
        return r

    def sumsq(srcs, W):
        bk = nb()
        n = len(srcs)
        for i, (ap, bufs) in enumerate(srcs):
            s = sq_ring()
            ACT(s.ap[:, 0:W], ap, AF.Square, bufs, [s.buf])
            MM(bk.ap[:, 0:W], ones_t[:, :], s.ap[:, 0:W], i == 0, i == n - 1, [s.buf, constb2], [bk.buf], inc=(i == n - 1))
        return bk

    NWARM = [int(x) for x in os.environ.get("MK_WARM", "0,0").split(",")]

    def warm(n):
        if n <= 0:
            return
        bk = nb()
        for i in range(n):
            MM(bk.ap[:, :], ones_t[:, :], mask1_t[:, 0:512], True, True, [constb2], [bk.buf], inc=True)

    def prenorm(l, ci, nA, nB, src_t, src_b, s0, W, d0):
        srcs = [(flat(src_t, c)[:, s0:s0 + W], colbufs(src_b, c, s0, s0 + W)) for c in range(8)]
        bk = sumsq(srcs, W)
        warm(NWARM[0])
        r = rstd_from(bk.ap[:, 0:W], bk.buf, W, 1024.0, EPS)
        for c in range(8):
            t = tmp_ring()
            STT(t.ap[:, 0:W], srcs[c][0], par(l, nA, c, ci), r.ap[:, 0:W], ALU.mult, ALU.mult,
                srcs[c][1] + [r.buf, parbuf(l, nA)], [t.buf])
            ACT(flat(hT, c)[:, d0:d0 + W], t.ap[:, 0:W], AF.Identity, [t.buf, parbuf(l, nB)], colbufs(hTb, c, d0, d0 + W),
                bias=par(l, nB, c, ci))

    POOL_ADD = True

    class EvacStats:
        def __init__(self, gs, banks_):
            self.bank = {g: b for g, b in zip(gs, banks_)}
            self.pend = None

        def add(self, bk, c, g, W):
            sq = sq_ring()
            ACT(sq.ap[:, 0:W], bk.ap[:, 0:W], AF.Square, [bk.buf], [sq.buf])
            self.flush()
            self.pend = (sq, c, g, W)

        def flush(self):
            if self.pend is None:
                return
            sq, c, g, W = self.pend
            sb_ = self.bank[g]
            MM(sb_.ap[:, 0:W], ones_t[:, :], sq.ap[:, 0:W], c == 0, c == 7, [sq.buf, constb2], [sb_.buf], inc=True)
            self.pend = None

    def postnorm(l, ci, nG, g, m0, W, x0, eps, stat=None):
        srcs = [(M[:, c, g, m0:m0 + W], [Mb[c][g]]) for c in range(8)]
        bk = stat if stat is not None else sumsq(srcs, W)
        warm(NWARM[1])
        r = rstd_from(bk.ap[:, 0:W], bk.buf, W, 1024.0, eps)
        for c in range(8):
            t = tmp_ring()
            STT(t.ap[:, 0:W], srcs[c][0], par(l, nG, c, ci), r.ap[:, 0:W], ALU.mult, ALU.mult,
                [Mb[c][g], r.buf, parbuf(l, nG)], [t.buf])
            if POOL_ADD and c % 2 == 1:
                S.op("pool", lambda e, o=xT[:, c, g, x0:x0 + W], a=xT[:, c, g, x0:x0 + W], b=t.ap[:, 0:W]: e.tensor_tensor(o, a, b, ALU.add),
                     [t.buf, xTb[c][g]], [xTb[c][g]])
            else:
                TT(xT[:, c, g, x0:x0 + W], xT[:, c, g, x0:x0 + W], t.ap[:, 0:W], ALU.add, [t.buf, xTb[c][g]], [xTb[c][g]])

    def proj_to_M(tagfmt, l, nchunks_per_piece, src_t, src_b, groups, after_group=None):
        if len(groups) > 1 and nchunks_per_piece == 4:
            b0 = next_piece(tagfmt % (l, 0))
            b1 = next_piece(tagfmt % (l, 1), hold=1)
            vs = [wview(b0, 8, 512), wview(b1, 8, 512)]
            bis = [b0, b1]
            set_rr(range(6))
            es = EvacStats([g for (g, s0, W) in groups], [bank(6), bank(7)])
            for (g, s0, W) in groups:
                for c in range(8):
                    v, bi, cc = vs[c // 4], bis[c // 4], c % 4
                    bk = nb()
                    items = [(v[:, k, cc * 128:(cc + 1) * 128], src_t[:, k, g, s0:s0 + W], [wbb[bi], src_b[k][g]]) for k in range(8)]
                    mm_group(bk.ap[:, 0:W], bk.buf, items)
                    ACT(M[:, c, g, 0:W], bk.ap[:, 0:W], AF.Copy, [bk.buf], [Mb[c][g]])
                    es.add(bk, c, g, W)
                es.flush()
                if after_group is not None:
                    after_group(g, es.bank[g])
            set_rr(range(8))
            return
        set_rr(range(6))
        es = EvacStats([g for (g, s0, W) in groups], [bank(6), bank(7)])
        for c in range(8):
            if c % nchunks_per_piece == 0:
                bi = next_piece(tagfmt % (l, c // nchunks_per_piece))
                v = wview(bi, 8, 512)
            cc = c % nchunks_per_piece
            for (g, s0, W) in groups:
                bk = nb()
                items = [(v[:, k, cc * 128:(cc + 1) * 128], src_t[:, k, g, s0:s0 + W], [wbb[bi], src_b[k][g]]) for k in range(8)]
                mm_group(bk.ap[:, 0:W], bk.buf, items)
                ACT(M[:, c, g, 0:W], bk.ap[:, 0:W], AF.Copy, [bk.buf], [Mb[c][g]])
                es.add(bk, c, g, W)
        es.flush()
        set_rr(range(8))
        return es

    def ffn(l, ci, groups, xoff, mid_hook=None, after_group=None):
        mh = None
        if mid_hook is not None:
            set_rr(range(7))
            mh = mid_hook(bank(7))
        for i in range(11):
            if mh is not None and i in (3, 5, 7, 9):
                next(mh, None)
            bi = next_piece("gu%d_%d" % (l, i))
            v = wview(bi, 8, 512)
            for (g, W) in groups:
                for jj in range(2):
                    j = i * 2 + jj
                    bg = nb()
                    items = [(v[:, k, jj * 128:(jj + 1) * 128], hT[:, k, g, 0:W], [wbb[bi], hTb[k][g]]) for k in range(8)]
                    mm_group(bg.ap[:, 0:W], bg.buf, items)
                    bu = nb()
                    items = [(v[:, k, 256 + jj * 128:256 + (jj + 1) * 128], hT[:, k, g, 0:W], [wbb[bi], hTb[k][g]]) for k in range(8)]
                    mm_group(bu.ap[:, 0:W], bu.buf, items)
                    t = tmp_ring()
                    ACT(t.ap[:, 0:W], bg.ap[:, 0:W], AF.Silu, [bg.buf], [t.buf])
                    hv = big[:, j * 1024 + g * 512: j * 1024 + g * 512 + W]
                    TT(hv, t.ap[:, 0:W], bu.ap[:, 0:W], ALU.mult, [t.buf, bu.buf], [hidb[j][g]])
        set_rr(range(6))
        es = EvacStats([g for (g, W) in groups], [bank(6), bank(7)])
        def down_unit(c, bi, g, W):
            v = wview(bi, JC, 128)
            bk = nb()
            items = [(v[:, j, :], big[:, j * 1024 + g * 512: j * 1024 + g * 512 + W], [wbb[bi], hidb[j][g]]) for j in range(JC)]
            mm_group(bk.ap[:, 0:W], bk.buf, items)
            ACT(M[:, c, g, 0:W], bk.ap[:, 0:W], AF.Copy, [bk.buf], [Mb[c][g]])
            es.add(bk, c, g, W)

        two = len(groups) == 2
        for c in range(6 if two else 8):
            bi = next_piece("dn%d_%d" % (l, c))
            for (g, W) in groups:
                down_unit(c, bi, g, W)
        if two:
            b6 = next_piece("dn%d_%d" % (l, 6))
            b7 = next_piece("dn%d_%d" % (l, 7), hold=1)
            for (g, W) in groups:
                down_unit(6, b6, g, W)
                down_unit(7, b7, g, W)
        es.flush()
        set_rr(range(8))
        for (g, W) in groups:
            postnorm(l, ci, 5, g, 0, W, xoff, EPS, stat=es.bank[g])
            if after_group is not None:
                after_group(g)

    def exp_recip_mul(dst_ap, dst_bufs, v_ap, v_bufs, W):
        t = tmp_ring()
        ACT(t.ap[:, 0:W], v_ap, AF.Exp, v_bufs, [t.buf], scale=-1.0)
        TS(t.ap[:, 0:W], t.ap[:, 0:W], 1.0, None, ALU.add, None, [t.buf], [t.buf])
        RECIP(t.ap[:, 0:W], t.ap[:, 0:W], [t.buf], [t.buf])
        TT(dst_ap, v_ap, t.ap[:, 0:W], ALU.mult, v_bufs + [t.buf], dst_bufs)

    allhid = [hidb[j][g] for j in range(JC) for g in range(2)]
    zb_v = big[:, 0:4576].rearrange("p (c s t) -> p c s t", c=4, s=4, t=286)
    zs_v = big[:, 0:4 * 542].rearrange("p (c t) -> p c t", c=4)
    q0_v = big[:, 4576:4576 + 4096].rearrange("p (c t) -> p c t", c=4)
    k0_v = big[:, 8672:8672 + 5120].rearrange("p (c t) -> p c t", c=4)
    v0_v = big[:, 13792:13792 + 5120].rearrange("p (t e) -> p t e", t=10)
    q1_v = big[:, 0:8192].rearrange("p (c t) -> p c t", c=8)
    k1_v = big[:, 8192:8192 + 5120].rearrange("p (c t) -> p c t", c=4)
    v1_v = big[:, 13312:13312 + 5120].rearrange("p (t k d e) -> p t k d e", t=10, k=4, d=2)
    zbuf_b = [Buf("z%d" % c) for c in range(4)]
    qb = [Buf("q%d" % c) for c in range(8)]
    kb = [Buf("k%d" % c) for c in range(4)]
    kctxb = Buf("kctx")
    vb = [Buf("v%d" % t) for t in range(10)]
    mixbufs = zbuf_b + qb + kb + [kctxb] + vb

    def rope_stage(bk, W, ctab, stab, dst_ap, dst_bufs):
        xbt = xb_ring()
        ACT(xbt.ap[:, 0:W], bk.ap[:, 0:W], AF.Copy, [bk.buf], [xbt.buf])

        def fin():
            b2 = nb()
            MM(b2.ap[:, 0:W], perm_t[:, :], xbt.ap[:, 0:W], True, True, [xbt.buf, constb2], [b2.buf], inc=True)
            ta = tmp_ring()
            tb_ = tmp_ring()
            TT(ta.ap[:, 0:W], bk.ap[:, 0:W], ctab[0], ALU.mult, [bk.buf] + ctab[1], [ta.buf])
            TT(tb_.ap[:, 0:W], b2.ap[:, 0:W], stab[0], ALU.mult, [b2.buf] + stab[1], [tb_.buf])
            TT(dst_ap, ta.ap[:, 0:W], tb_.ap[:, 0:W], ALU.add, [ta.buf, tb_.buf], dst_bufs)
        return fin

    DGRP = [(0, 8), (8, 16), (16, 24), (24, 31)]
    diag_gb = [Buf("diagg%d" % i) for i in range(4)]

    class DG:
        def __init__(self):
            self.ap = diag_t[:, 0, :, :]

        def buf(self, j):
            return diag_gb[min(j // 8, 3)]

    def conv_build_group(c, gi):
        dg = DG()
        ia = ident_t[:, :]
        j0, j1 = DGRP[gi]
        n = j1 - j0
        wa = convw_t[:, c * 31 + j0:c * 31 + j1]
        in0 = bass.AP(ia.tensor, ia.offset, [list(ia.ap[0]), [0, n], [1, 128]])
        in1 = bass.AP(wa.tensor, wa.offset, [list(wa.ap[0]), [1, n], [0, 128]])
        TT(dg.ap[:, j0:j1, :], in0, in1, ALU.mult, CB, [diag_gb[gi]])

    def conv_build_diag(c):
        for gi in range(4):
            conv_build_group(c, gi)
        return DG()

    LAST_TAP = {7: 0, 15: 1, 23: 2, 30: 3}

    def conv_ln_silu(g, W):
        b1 = nb()
        b2 = nb()
        for c in range(4):
            xbt = xb_ring()
            ACT(xbt.ap[:, 0:W], M[:, c, g, 0:W], AF.Copy, [Mb[c][g]], [xbt.buf])
            MM(b1.ap[:, 0:W], ones_t[:, :], xbt.ap[:, 0:W], c == 0, c == 3, [xbt.buf, constb2], [b1.buf], inc=(c == 3))
        for c in range(4):
            s = sq_ring()
            ACT(s.ap[:, 0:W], M[:, c, g, 0:W], AF.Square, [Mb[c][g]], [s.buf])
            MM(b2.ap[:, 0:W], ones_t[:, :], s.ap[:, 0:W], c == 0, c == 3, [s.buf, constb2], [b2.buf], inc=(c == 3))
        mean = mean_ring()
        TS(mean.ap[:, 0:W], b1.ap[:, 0:W], 1.0 / 512.0, None, ALU.mult, None, [b1.buf], [mean.buf])
        t = tmp_ring()
        TT(t.ap[:, 0:W], mean.ap[:, 0:W], mean.ap[:, 0:W], ALU.mult, [mean.buf], [t.buf])
        var = tmp_ring()
        STT(var.ap[:, 0:W], b2.ap[:, 0:W], 1.0 / 512.0, t.ap[:, 0:W], ALU.mult, ALU.subtract, [b2.buf, t.buf], [var.buf])
        t2 = tmp_ring()
        r = rstd_ring()
        ACT(t2.ap[:, 0:W], var.ap[:, 0:W], AF.Ln, [var.buf], [t2.buf], bias=EPS)
        ACT(r.ap[:, 0:W], t2.ap[:, 0:W], AF.Exp, [t2.buf], [r.buf], scale=-0.5)
        for c in range(4):
            u = tmp_ring()
            TT(u.ap[:, 0:W], M[:, c, g, 0:W], mean.ap[:, 0:W], ALU.subtract, [Mb[c][g], mean.buf], [u.buf])
            TT(u.ap[:, 0:W], u.ap[:, 0:W], r.ap[:, 0:W], ALU.mult, [u.buf, r.buf], [u.buf])
            ACT(hT[:, c, g, 0:W], u.ap[:, 0:W], AF.Silu, [u.buf] + CB, [hTb[c][g]],
                scale=lng_t[:, c:c + 1], bias=lnb_t[:, c:c + 1])

    def subln_out(g, W):
        for h in range(4):
            bk = sumsq([(M[:, 4 + h, g, 0:W], [Mb[4 + h][g]])], W)
            r = rstd_from(bk.ap[:, 0:W], bk.buf, W, 128.0, EPS)
            t = tmp_ring()
            TT(t.ap[:, 0:W], M[:, 4 + h, g, 0:W], r.ap[:, 0:W], ALU.mult, [Mb[4 + h][g], r.buf], [t.buf])
            TS(hT[:, 4 + h, g, 0:W], t.ap[:, 0:W], sgs, None, ALU.mult, None, [t.buf, lamb], [hTb[4 + h][g]])

    def recip_act(dst, src_ap, src_bufs, W, bias=0.0, extra_reads=()):
        ACT(dst.ap[:, 0:W], src_ap, AF.Ln, list(src_bufs) + list(extra_reads), [dst.buf], bias=bias)
        ACT(dst.ap[:, 0:W], dst.ap[:, 0:W], AF.Exp, [dst.buf], [dst.buf], scale=-1.0)

    def glu(ba, bg, W, outs):
        t = tmp_ring()
        ACT(t.ap[:, 0:W], bg.ap[:, 0:W], AF.Sigmoid, [bg.buf], [t.buf])
        for (dst_ap, a0, a1, dbufs, shp) in outs:
            a_ap = ba.ap[:, a0:a1]
            t_ap = t.ap[:, a0:a1]
            if shp is not None:
                a_ap = a_ap.rearrange("p (s t) -> p s t", s=shp)
                t_ap = t_ap.rearrange("p (s t) -> p s t", s=shp)
            TT(dst_ap, a_ap, t_ap, ALU.mult, [ba.buf, t.buf], dbufs)

    def store_y(g_list):
        for (g, x0, W, ycol) in g_list:
            for c in range(8):
                S.dma("sp", yT_d[c * 128:(c + 1) * 128, ycol:ycol + W], xT[:, c, g, x0:x0 + W],
                      reads=[xTb[c][g]], final=True, track=ybuf[ycol // 512])

    ybuf = [Buf("yout%d" % i) for i in range(3)]

    def prompt_layer0(mid=None, fuse_next=False):
        ci = 0
        l = 0
        for g in range(2):
            prenorm(l, ci, 1, 0, xT, xTb, g * 512, 512, g * 512)
        handoff(allhid, mixbufs)
        ckpt("p0_norm")
        S.op("dve", lambda e: e.memset(big[:, 0:4576], 0.0), [], zbuf_b)
        for i in range(2):
            bi = next_piece("glu%d" % i)
            v = wview(bi, 8, 512)
            for g in range(2):
                for cc in range(2):
                    c = i * 2 + cc
                    ba = nb()
                    mm_group(ba.ap[:, :], ba.buf, [(v[:, k, cc * 128:(cc + 1) * 128], hT[:, k, g, :], [wbb[bi], hTb[k][g]]) for k in range(8)])
                    bg = nb()
                    mm_group(bg.ap[:, :], bg.buf, [(v[:, k, 256 + cc * 128:256 + (cc + 1) * 128], hT[:, k, g, :], [wbb[bi], hTb[k][g]]) for k in range(8)])
                    glu(ba, bg, 512, [(zb_v[:, c, 2 * g:2 * g + 2, 15:271], 0, 512, [zbuf_b[c]], 2)])
        ckpt("p0_glu")
        bi = next_piece("q0")
        v = wview(bi, 8, 512)
        for c in range(4):
            for g in range(2):
                bk = nb()
                mm_group(bk.ap[:, :], bk.buf, [(v[:, k, c * 128:(c + 1) * 128], hT[:, k, g, :], [wbb[bi], hTb[k][g]]) for k in range(8)])
                ACT(q0_v[:, c, g * 512:(g + 1) * 512], bk.ap[:, :], AF.Copy, [bk.buf], [qb[c]])
        ckpt("p0_q")
        bi = next_piece("k0")
        v = wview(bi, 8, 512)
        for c in range(4):
            for g in range(2):
                bk = nb()
                mm_group(bk.ap[:, :], bk.buf, [(v[:, k, c * 128:(c + 1) * 128], hT[:, k, g, :], [wbb[bi], hTb[k][g]]) for k in range(8)])
                st = stg_ring()
                ACT(st.ap[:, :], bk.ap[:, :], AF.Copy, [bk.buf], [st.buf])
                S.dma("sp", k0T_d[c * 128:(c + 1) * 128, g * 512:(g + 1) * 512], st.ap[:, :], reads=[st.buf], final=True)
                COPYV(k0_v[:, c, g * 512:(g + 1) * 512], st.ap[:, :], [st.buf], [kb[c]])
        ckpt("p0_k")
        bi = next_piece("v0")
        v = wview(bi, 8, 512)
        for tl in range(8):
            g, t0_ = tl // 4, (tl % 4) * 128
            bk = nb()
            mm_group(bk.ap[:, :], bk.buf, [(hT[:, k, g, t0_:t0_ + 128], v[:, k, :], [wbb[bi], hTb[k][g]]) for k in range(8)])
            st = stg_ring()
            ACT(st.ap[:, :], bk.ap[:, :], AF.Copy, [bk.buf], [st.buf])
            S.dma("sp", v0_d[tl * 128:(tl + 1) * 128, :], st.ap[:, :], reads=[st.buf], final=True)
            COPYV(v0_v[:, tl, :], st.ap[:, :], [st.buf], [vb[tl]])
        ckpt("p0_v")
        mg = mid(bank(1)) if mid is not None else None
        steps = [(h, s) for h in range(4) for s in range(4)]
        s_banks = Ring([bank(i) for i in (2, 3, 4, 5)])
        conv_banks = [bank(0), bank(0)]
        stA = {}

        def att_A(h, s):
            pp = pair_ring()
            ppv = pp.ap.rearrange("p (t q) -> p t q", t=2)
            for j in range(2):
                bs = s_banks()
                for t in range(2):
                    kc0 = s * 256 + t * 128
                    MM(bs.ap[:, t * 256:(t + 1) * 256], k0_v[j * 64:(j + 1) * 64, h, kc0:kc0 + 128],
                       q0_v[j * 64:(j + 1) * 64, h, s * 256:(s + 1) * 256], True, True, [kb[h], qb[h]], [bs.buf], inc=True)
                ACT(ppv[:, :, j * 256:(j + 1) * 256], bs.ap[:, :].rearrange("p (t q) -> p t q", t=2), AF.Exp, [bs.buf], [pp.buf], scale=0.125)
            stA[(h, s)] = pp

        def att_B(h, s):
            g, so = s // 2, (s % 2) * 256
            pp = stA.pop((h, s))
            bo = bank(6)
            bd = bank(7)
            for t in range(2):
                tl = s * 2 + t
                MM(bo.ap[:, :], v0_v[:, tl, h * 128:(h + 1) * 128], pp.ap[:, t * 512:(t + 1) * 512],
                   t == 0, t == 1, [vb[tl], pp.buf], [bo.buf], inc=True)
            for t in range(2):
                MM(bd.ap[:, :], ones_t[:, :], pp.ap[:, t * 512:(t + 1) * 512],
                   t == 0, t == 1, [constb2, pp.buf], [bd.buf], inc=True)
            r = tmp_ring()
            recip_act(r, bd.ap[:, :], [bd.buf], 512)
            on = tmp_ring()
            TT(on.ap[:, :], bo.ap[:, :], r.ap[:, :], ALU.mult, [bo.buf, r.buf], [on.buf])
            STT(M[:, 4 + h, g, so:so + 256], on.ap[:, 256:512], neglam, on.ap[:, 0:256], ALU.mult, ALU.add,
                [on.buf, lamb], [Mb[4 + h][g]])

        dgs = {}

        def conv_step(c, s):
            g, si = s // 2, s % 2
            if s == 0 and c == 0:
                dgs[c] = conv_build_diag(c)
            dg = dgs[c]
            bk = conv_banks[g]
            for j in range(31):
                MM(bk.ap[:, si * 256:(si + 1) * 256], dg.ap[:, j, :], zb_v[:, c, s, j:j + 256], j == 0, j == 30,
                   [dg.buf(j), zbuf_b[c]], [bk.buf], inc=True)
                if s == 3 and c < 3 and j in LAST_TAP:
                    conv_build_group(c + 1, LAST_TAP[j])
                    dgs[c + 1] = DG()
            if si == 1:
                ACT(M[:, c, g, :], bk.ap[:, :], AF.Identity, [bk.buf] + CB, [Mb[c][g]], bias=convb_t[:, c:c + 1])

        att_A(*steps[0])
        for i, (h, s) in enumerate(steps):
            if i + 1 < len(steps):
                att_A(*steps[i + 1])
            conv_step(h, s)
            if mg is not None and i % 2 == 1:
                next(mg, None)
            att_B(h, s)
        if mg is not None:
            for _ in mg:
                pass
        set_rr(range(8))
        ckpt("p0_attn")
        for g in range(2):
            subln_out(g, 512)
            conv_ln_silu(g, 512)
        ckpt("p0_subln")
        def after_out(g, stat):
            postnorm(l, ci, 2, g, 0, 512, 0, EPS, stat=stat)
            prenorm(l, ci, 4, 3, xT, xTb, g * 512, 512, g * 512)

        proj_to_M("out%d_%d", l, 4, hT, hTb, [(0, 0, 512), (1, 0, 512)], after_group=after_out)
        ckpt("p0_out")
        handoff(mixbufs, allhid)
        ckpt("p0_norm2")
        if fuse_next:
            ffn(l, ci, [(0, 512), (1, 512)], 0, mid_hook=lambda bk: mod_steps(1, 0, bk),
                after_group=lambda g: prenorm(1, ci, 1, 0, xT, xTb, g * 512, 512, g * 512))
        else:
            ffn(l, ci, [(0, 512), (1, 512)], 0)

    def prompt_layer1(mid=None, skip_prenorm=False):
        ci = 0
        l = 1
        if not skip_prenorm:
            for g in range(2):
                prenorm(l, ci, 1, 0, xT, xTb, g * 512, 512, g * 512)
        handoff(allhid, mixbufs)
        for i in range(2):
            bi = next_piece("q1_%d" % i)
            v = wview(bi, 8, 512)
            for g in range(2):
                for cc in range(4):
                    c = i * 4 + cc
                    bk = nb()
                    mm_group(bk.ap[:, :], bk.buf, [(v[:, k, cc * 128:(cc + 1) * 128], hT[:, k, g, :], [wbb[bi], hTb[k][g]]) for k in range(8)])
                    ACT(q1_v[:, c, g * 512:(g + 1) * 512], bk.ap[:, :], AF.Copy, [bk.buf], [qb[c]])
        bi = next_piece("k1")
        v = wview(bi, 8, 512)
        for kv in range(4):
            for g in range(2):
                bk = nb()
                mm_group(bk.ap[:, :], bk.buf, [(v[:, k, kv * 128:(kv + 1) * 128], hT[:, k, g, :], [wbb[bi], hTb[k][g]]) for k in range(8)])
                st = stg_ring()
                ACT(st.ap[:, :], bk.ap[:, :], AF.Copy, [bk.buf], [st.buf])
                S.dma("sp", k1T_d[kv * 64:(kv + 1) * 64, g * 512:(g + 1) * 512], st.ap[0:64, :], reads=[st.buf], final=True)
                COPYV(k1_v[:, kv, g * 512:(g + 1) * 512], st.ap[:, :], [st.buf], [kb[kv]])
        bi = next_piece("v1")
        v = wview(bi, 8, 256)
        for tl in range(8):
            g, t0_ = tl // 4, (tl % 4) * 128
            bk = nb()
            mm_group(bk.ap[:, 0:256], bk.buf, [(hT[:, k, g, t0_:t0_ + 128], v[:, k, :], [wbb[bi], hTb[k][g]]) for k in range(8)])
            st = stg_ring()
            ACT(st.ap[:, 0:256], bk.ap[:, 0:256], AF.Copy, [bk.buf], [st.buf])
            S.dma("sp", v1_d[tl * 128:(tl + 1) * 128, :], st.ap[:, 0:256], reads=[st.buf], final=True)
            for dup in range(2):
                COPYV(v1_v[:, tl, :, dup, :], st.ap[:, 0:256].rearrange("p (k e) -> p k e", k=4), [st.buf], [vb[tl]])
        mg = mid(bank(7)) if mid is not None else None
        s_banks = Ring([bank(i) for i in (0, 1, 2)])
        o_banks = Ring([bank(i) for i in (3, 4)])
        d_banks = Ring([bank(i) for i in (5, 6)])
        steps = [(s, c) for s in range(4) for c in range(8)]
        stA = {}

        def att_A(s, c):
            kv = c // 2
            pp = pair_ring()
            ppv = pp.ap.rearrange("p (t q) -> p t q", t=2)
            for hp in range(2):
                pr = slice(hp * 64, (hp + 1) * 64)
                bs = s_banks()
                for t in range(2):
                    kc0 = s * 256 + t * 128
                    MM(bs.ap[:, t * 256:(t + 1) * 256], k1_v[pr, kv, kc0:kc0 + 128], q1_v[pr, c, s * 256:(s + 1) * 256],
                       True, True, [kb[kv], qb[c]], [bs.buf], inc=True)
                ACT(ppv[:, :, hp * 256:(hp + 1) * 256], bs.ap[:, :].rearrange("p (t q) -> p t q", t=2), AF.Exp, [bs.buf], [pp.buf], scale=0.125)
            stA[(s, c)] = pp

        def att_B(s, c):
            g, so = s // 2, (s % 2) * 256
            kv = c // 2
            pp = stA.pop((s, c))
            bo, bd = o_banks(), d_banks()
            for t in range(2):
                tl = s * 2 + t
                MM(bo.ap[:, :], v1_v[:, tl, kv, :, :].rearrange("p d e -> p (d e)"), pp.ap[:, t * 512:(t + 1) * 512],
                   t == 0, t == 1, [vb[tl], pp.buf], [bo.buf], inc=True)
            for t in range(2):
                MM(bd.ap[:, :], ones_t[:, :], pp.ap[:, t * 512:(t + 1) * 512],
                   t == 0, t == 1, [constb2, pp.buf], [bd.buf], inc=True)
            r = tmp_ring()
            for h2 in range(2):
                pr = slice(h2 * 64, (h2 + 1) * 64)
                ACT(r.ap[pr, 0:256], bd.ap[pr, h2 * 256:(h2 + 1) * 256], AF.Ln, [bd.buf, sinkb], [r.buf], bias=sink_t[pr, c:c + 1])
            ACT(r.ap[:, 0:256], r.ap[:, 0:256], AF.Exp, [r.buf], [r.buf], scale=-1.0)
            for h2 in range(2):
                pr = slice(h2 * 64, (h2 + 1) * 64)
                TT(hT[pr, c, g, so:so + 256], bo.ap[pr, h2 * 256:(h2 + 1) * 256], r.ap[pr, 0:256], ALU.mult, [bo.buf, r.buf], [hTb[c][g]])

        att_A(*steps[0])
        for i, st_ in enumerate(steps):
            if i + 1 < len(steps):
                att_A(*steps[i + 1])
            if mg is not None and i % 4 == 3:
                next(mg, None)
            att_B(*st_)
        if mg is not None:
            for _ in mg:
                pass
        set_rr(range(8))

        def after_out(g, stat):
            postnorm(l, ci, 2, g, 0, 512, 0, EPS, stat=stat)
            prenorm(l, ci, 4, 3, xT, xTb, g * 512, 512, g * 512)

        proj_to_M("out%d_%d", l, 4, hT, hTb, [(0, 0, 512), (1, 0, 512)], after_group=after_out)
        handoff(mixbufs, allhid)
        ffn(l, ci, [(0, 512), (1, 512)], 0)

    xsa_loaded = []

    def load_xsa(g):
        S.dma("sp", M[:, :, g, :], xsa_d[:, g * 512:(g + 1) * 512].rearrange("(c p) t -> p c t", p=128),
              writes=[Mb[c][g] for c in range(8)])

    def sample_layer0(mid=None):
        ci = 1
        l = 0
        for g in range(2):
            if not xsa_loaded:
                load_xsa(g)
            prenorm(l, ci, 1, 0, M, Mb, g * 512, 512, g * 512)
        handoff(allhid, mixbufs)
        S.dma("sp", flat(M, 0), ropeAc_d[:, :], writes=[Mb[0][0], Mb[0][1]])
        S.dma("sp", flat(M, 1), ropeAs_d[:, :], writes=[Mb[1][0], Mb[1][1]])
        for c in range(4):
            S.dma("pool", k0_v[:, c, 1024:1280], ck0T_d[c * 128:(c + 1) * 128, :], writes=[kctxb])
        for t in range(2):
            S.dma("pool", v0_v[:, 8 + t, :], cv0_d[t * 128:(t + 1) * 128, :], writes=[vb[8 + t]])
        bi = next_piece("k0")
        v = wview(bi, 8, 512)
        pend = None
        for c in range(4):
            for g in range(2):
                bk = nb()
                mm_group(bk.ap[:, :], bk.buf, [(v[:, k, c * 128:(c + 1) * 128], hT[:, k, g, :], [wbb[bi], hTb[k][g]]) for k in range(8)])
                fin = rope_stage(bk, 512, (M[:, 0, g, :], [Mb[0][g]]), (M[:, 1, g, :], [Mb[1][g]]),
                                 k0_v[:, c, g * 512:(g + 1) * 512], [kb[c]])
                if pend is not None:
                    pend()
                pend = fin
        pend()
        bi = next_piece("v0")
        v = wview(bi, 8, 512)
        for tl in range(8):
            g, t0_ = tl // 4, (tl % 4) * 128
            bk = nb()
            mm_group(bk.ap[:, :], bk.buf, [(hT[:, k, g, t0_:t0_ + 128], v[:, k, :], [wbb[bi], hTb[k][g]]) for k in range(8)])
            ACT(v0_v[:, tl, :], bk.ap[:, :], AF.Copy, [bk.buf], [vb[tl]])
        S.dma("sp", xT[:, :, 0, :], xse_d[:, 0:512].rearrange("(c p) t -> p c t", p=128), writes=[xTb[c][0] for c in range(8)])
        S.dma("sp", xT[:, :, 1, 0:30], xse_d[:, 512:542].rearrange("(c p) t -> p c t", p=128), writes=[xTb[c][1] for c in range(8)])
        prenorm(l, ci, 1, 0, xT, xTb, 0, 512, 0)
        prenorm(l, ci, 1, 0, xT, xTb, 512, 30, 512)
        for i in range(2):
            bi = next_piece("glu%d" % i)
            v = wview(bi, 8, 512)
            for cc in range(2):
                c = i * 2 + cc
                for (s0, W, segs) in ((0, 512, [(0, 512, 15)]), (512, 30, [(0, 15, 0), (15, 30, 527)])):
                    ba = nb()
                    mm_group(ba.ap[:, 0:W], ba.buf, [(v[:, k, cc * 128:(cc + 1) * 128], flat(hT, k)[:, s0:s0 + W], [wbb[bi]] + colbufs(hTb, k, s0, s0 + W)) for k in range(8)])
                    bg = nb()
                    mm_group(bg.ap[:, 0:W], bg.buf, [(v[:, k, 256 + cc * 128:256 + (cc + 1) * 128], flat(hT, k)[:, s0:s0 + W], [wbb[bi]] + colbufs(hTb, k, s0, s0 + W)) for k in range(8)])
                    glu(ba, bg, W, [(zs_v[:, c, z0:z0 + (a1 - a0)], a0, a1, [zbuf_b[c]], None) for (a0, a1, z0) in segs])
                TT(zs_v[:, c, :], zs_v[:, c, :], valid_t[:, :], ALU.mult, [zbuf_b[c], constb2], [zbuf_b[c]])
        bi = next_piece("q0")
        v = wview(bi, 8, 512)
        pend = None
        ectab = (ropeEc_t[:, :], CB)
        estab = (ropeEs_t[:, :], CB)
        for c in range(4):
            bk = nb()
            mm_group(bk.ap[:, :], bk.buf, [(v[:, k, c * 128:(c + 1) * 128], hT[:, k, 0, :], [wbb[bi], hTb[k][0]]) for k in range(8)])
            fin = rope_stage(bk, 512, ectab, estab, q0_v[:, c, 0:512], [qb[c]])
            if pend is not None:
                pend()
            pend = fin
        pend()
        if mid is not None:
            for _ in mid(None):
                pass
        s_banks = Ring([bank(i) for i in (0, 1, 2)])
        for h in range(4):
            c = h
            dg = conv_build_diag(c) if c == 0 else DG()
            cbk = bank(3)
            conv_j = [0]

            def conv_some(n, c=c, dg=dg, cbk=cbk, conv_j=conv_j):
                for _ in range(n):
                    j = conv_j[0]
                    if j >= 31:
                        return
                    MM(cbk.ap[:, :], dg.ap[:, j, :], zs_v[:, c, j:j + 512], j == 0, j == 30, [dg.buf(j), zbuf_b[c]], [cbk.buf], inc=True)
                    if c < 3 and j in LAST_TAP:
                        conv_build_group(c + 1, LAST_TAP[j])
                    conv_j[0] += 1

            bo = [bank(4), bank(5)]
            bd = [bank(6), bank(7)]
            for j in range(2):
                pr = slice(j * 64, (j + 1) * 64)
                pend_p = []
                for t in range(12):
                    if t < 10:
                        bs = s_banks()
                        kbuf = kb[h] if t < 8 else kctxb
                        MM(bs.ap[:, :], k0_v[pr, h, t * 128:(t + 1) * 128], q0_v[pr, h, 0:512], True, True, [kbuf, qb[h]], [bs.buf], inc=True)
                        p = p6_ring()
                        ACT(p.ap[:, 0:512], bs.ap[:, :], AF.Exp, [bs.buf], [p.buf], scale=0.125)
                        pend_p.append((t, p))
                        conv_some(2)
                    if t >= 2:
                        tp, pp = pend_p.pop(0)
                        MM(bo[j].ap[:, :], v0_v[:, tp, h * 128:(h + 1) * 128], pp.ap[:, 0:512], tp == 0, tp == 9, [vb[tp], pp.buf], [bo[j].buf], inc=True)
                        MM(bd[j].ap[:, :], ones_t[:, :], pp.ap[:, 0:512], tp == 0, tp == 9, [constb2, pp.buf], [bd[j].buf], inc=True)
            conv_some(31)
            ACT(M[:, c, 0, :], cbk.ap[:, :], AF.Identity, [cbk.buf] + CB, [Mb[c][0]], bias=convb_t[:, c:c + 1])
            on = []
            for j in range(2):
                r = tmp_ring()
                recip_act(r, bd[j].ap[:, :], [bd[j].buf], 512)
                o = tmp_ring()
                TT(o.ap[:, :], bo[j].ap[:, :], r.ap[:, :], ALU.mult, [bo[j].buf, r.buf], [o.buf])
                on.append(o)
            STT(M[:, 4 + h, 0, :], on[1].ap[:, :], neglam, on[0].ap[:, :], ALU.mult, ALU.add, [on[0].buf, on[1].buf, lamb], [Mb[4 + h][0]])
        set_rr(range(8))
        subln_out(0, 512)
        conv_ln_silu(0, 512)
        es_ = proj_to_M("out%d_%d", l, 4, hT, hTb, [(0, 0, 512)])
        postnorm(l, ci, 2, 0, 0, 512, 0, EPS, stat=es_.bank[0])
        prenorm(l, ci, 4, 3, xT, xTb, 0, 512, 0)
        handoff(mixbufs, allhid)
        ffn(l, ci, [(0, 512)], 0)

    def sample_layer1(mid=None):
        ci = 1
        l = 1
        prenorm(l, ci, 1, 0, xT, xTb, 0, 512, 0)
        handoff(allhid, mixbufs)
        for kv in range(4):
            S.dma("pool", k1_v[:, kv, 512:768], ck1T_d[kv * 128:(kv + 1) * 128, :], writes=[kctxb])
        for t in range(2):
            for dup in range(2):
                S.dma("pool", v1_v[:, 4 + t, :, dup, :], cv1_d[t * 128:(t + 1) * 128, :].rearrange("p (k e) -> p k e", k=4), writes=[vb[4 + t]])
        ckpt("s1_a")
        ectq = (ropeEc_t[:, 128:384], CB)
        estq = (ropeEs_t[:, 128:384], CB)
        ectab = (ropeEc_t[:, :], CB)
        estab = (ropeEs_t[:, :], CB)
        pend = None
        for i in range(2):
            bi = next_piece("q1_%d" % i)
            v = wview(bi, 8, 512)
            for cc in range(4):
                c = i * 4 + cc
                bk = nb()
                mm_group(bk.ap[:, 0:256], bk.buf, [(v[:, k, cc * 128:(cc + 1) * 128], hT[:, k, 0, 128:384], [wbb[bi], hTb[k][0]]) for k in range(8)])
                fin = rope_stage(bk, 256, ectq, estq, q1_v[:, c, 0:256], [qb[c]])
                if pend is not None:
                    pend()
                pend = fin
        bi = next_piece("k1")
        v = wview(bi, 8, 512)
        for kv in range(4):
            bk = nb()
            mm_group(bk.ap[:, :], bk.buf, [(v[:, k, kv * 128:(kv + 1) * 128], hT[:, k, 0, :], [wbb[bi], hTb[k][0]]) for k in range(8)])
            fin = rope_stage(bk, 512, ectab, estab, k1_v[:, kv, 0:512], [kb[kv]])
            pend()
            pend = fin
        pend()
        bi = next_piece("v1")
        v = wview(bi, 8, 256)
        for tl in range(4):
            bk = nb()
            mm_group(bk.ap[:, 0:256], bk.buf, [(hT[:, k, 0, tl * 128:(tl + 1) * 128], v[:, k, :], [wbb[bi], hTb[k][0]]) for k in range(8)])
            for dup in range(2):
                ACT(v1_v[:, tl, :, dup, :], bk.ap[:, 0:256].rearrange("p (k e) -> p k e", k=4), AF.Copy, [bk.buf], [vb[tl]])
        if mid is not None:
            for _ in mid(None):
                pass
        s_banks = Ring([bank(i) for i in (0, 1, 2, 3)])
        ma = mask1_t[:, :]
        pend_p = []

        def att_pv(c, kt, pp):
            kv = c // 2
            bo, bd = bank(4 + c % 2), bank(6 + c % 2)
            MM(bo.ap[:, :], v1_v[:, kt, kv, :, :].rearrange("p d e -> p (d e)"), pp.ap[:, 0:512],
               kt == 0, kt == 5, [vb[kt], pp.buf], [bo.buf], inc=True)
            MM(bd.ap[:, :], ones_t[:, :], pp.ap[:, 0:512], kt == 0, kt == 5, [constb2, pp.buf], [bd.buf], inc=True)
            if kt == 5:
                r = tmp_ring()
                for h2 in range(2):
                    pr = slice(h2 * 64, (h2 + 1) * 64)
                    ACT(r.ap[pr, 0:256], bd.ap[pr, h2 * 256:(h2 + 1) * 256], AF.Ln, [bd.buf, sinkb], [r.buf], bias=sink_t[pr, c:c + 1])
                ACT(r.ap[:, 0:256], r.ap[:, 0:256], AF.Exp, [r.buf], [r.buf], scale=-1.0)
                for h2 in range(2):
                    pr = slice(h2 * 64, (h2 + 1) * 64)
                    TT(hT[pr, c, 0, 0:256], bo.ap[pr, h2 * 256:(h2 + 1) * 256], r.ap[pr, 0:256], ALU.mult, [bo.buf, r.buf], [hTb[c][0]])

        for c in range(8):
            kv = c // 2
            for kt in range(6):
                kbuf = kb[kv] if kt < 4 else kctxb
                pp = p6_ring()
                for hp in range(2):
                    pr = slice(hp * 64, (hp + 1) * 64)
                    bs = s_banks()
                    MM(bs.ap[:, 0:256], k1_v[pr, kv, kt * 128:(kt + 1) * 128], q1_v[pr, c, 0:256],
                       True, True, [kbuf, qb[c]], [bs.buf], inc=True)
                    ACT(pp.ap[:, hp * 256:(hp + 1) * 256], bs.ap[:, 0:256], AF.Exp, [bs.buf], [pp.buf], scale=0.125)
                if kt < 4:
                    mk = bass.AP(ma.tensor, ma.offset + kt * 256, [list(ma.ap[0]), [0, 2], [1, 256]])
                    TT(pp.ap[:, 0:512].rearrange("p (h q) -> p h q", h=2), pp.ap[:, 0:512].rearrange("p (h q) -> p h q", h=2),
                       mk, ALU.mult, [pp.buf, constb2], [pp.buf])
                pend_p.append((c, kt, pp))
                if len(pend_p) > 2:
                    att_pv(*pend_p.pop(0))
        while pend_p:
            att_pv(*pend_p.pop(0))
        set_rr(range(8))
        ckpt("s1_attn")
        es_ = proj_to_M("out%d_%d", l, 4, hT, hTb, [(0, 0, 256)])
        postnorm(l, ci, 2, 0, 0, 256, 128, EPS, stat=es_.bank[0])
        ckpt("s1_out")
        prenorm(l, ci, 4, 3, xT, xTb, 128, 256, 0)
        handoff(mixbufs, allhid)
        ffn(l, ci, [(0, 256)], 128)

    try:
        if sample_first:
            do_mod(0, 0)
            sample_layer0(mid=lambda bk: mod_steps(0, 1, bk))
            ckpt("f_s0")
            do_mod(1, 0)
            sample_layer1(mid=lambda bk: mod_steps(1, 1, bk))
            store_y([(0, 128, 256, 1024)])
            ckpt("f_s1")
            raise StopBuild("end")
        ckpt("setup")
        for g in range(2):
            S.dma("sp", xT[:, :, g, :], xp_d[:, g * 512:(g + 1) * 512].rearrange("(c p) t -> p c t", p=128),
                  writes=[xTb[c][g] for c in range(8)])
        do_mod(0, 0)
        ckpt("mod0")
        prompt_layer0(mid=lambda bk: mod_steps(0, 1, bk), fuse_next=True)
        ckpt("p0")
        prompt_layer1(mid=lambda bk: mod_steps(1, 1, bk), skip_prenorm=True)
        ckpt("p1")
        load_xsa(0)
        store_y([(0, 0, 512, 0)])
        load_xsa(1)
        store_y([(1, 0, 512, 512)])
        xsa_loaded.append(True)
        sample_layer0()
        ckpt("s0")
        sample_layer1()
        store_y([(0, 128, 256, 1024)])
    except StopBuild:
        pass

    fin_waits = list(S.final.items())
    sp = S.eng["sp"]
    sp.ops.append((fin_waits, None, 0, None))

    with nc.Block() as block:
        @block.tensor
        def _(e):
            S.replay("pe", e)

        @block.scalar
        def _(e):
            S.replay("act", e)

        @block.vector
        def _(e):
            S.replay("dve", e)

        @block.gpsimd
        def _(e):
            S.replay("pool", e)

        @block.sync
        def _(e):
            eng = S.eng["sp"]
            for waits, fn, inc, sem in eng.ops:
                for s, v in waits:
                    e.wait_ge(s.h, v)
                if fn is None:
                    continue
                ins = fn(e)
                if inc:
                    ins.then_inc(sem.h, inc)
    stack.close()
    return nc


def _rope_tables(pos):
    p = np.arange(128)
    q = p % 64
    half = q // 32
    idx = q % 32
    part = idx // 16
    i = idx % 16
    inv = (1.0 / (ROPE_THETA ** (np.arange(0, 32, 2, dtype=np.float32) / np.float32(32)))).astype(np.float32)
    rows = np.floor_divide(pos, 64).astype(np.float32)
    cols = np.mod(pos, 64).astype(np.float32)
    posv = np.where(half[:, None] == 0, rows[None, :], cols[None, :]).astype(np.float32)
    ang = (posv * inv[i][:, None]).astype(np.float32)
    c = np.cos(ang).astype(np.float32)
    s = np.sin(ang).astype(np.float32)
    s = np.where(part[:, None] == 0, -s, s).astype(np.float32)
    return np.ascontiguousarray(c), np.ascontiguousarray(s)


def _pp(vec, nch):
    return np.ascontiguousarray(np.asarray(vec, np.float32).reshape(nch, 128).T)


_NC_CACHE = {}


def _prep(x_prompt, x_sample, cache_k0, cache_v0, cache_k1, cache_v1, c, c_ctx,
          l0_mod_w, l0_mod_b, l0_norm_g, l0_w_in, l0_conv_w, l0_conv_b, l0_conv_ln_g,
          l0_conv_ln_b, l0_lambda, l0_subln_g, l0_w_out, l0_w_gu, l0_w_down,
          l1_mod_w, l1_mod_b, l1_norm_g, l1_w_qkv, l1_sink, l1_w_out, l1_w_gu, l1_w_down):
    f = lambda a: np.ascontiguousarray(np.asarray(a, dtype=np.float32))
    x_prompt, x_sample = f(x_prompt), f(x_sample)
    cache_k0, cache_v0, cache_k1, cache_v1 = f(cache_k0), f(cache_v0), f(cache_k1), f(cache_v1)
    c, c_ctx = f(c), f(c_ctx)

    shared = {
        "modw0": f(l0_mod_w), "modw1": f(l1_mod_w),
        "gn0": np.ascontiguousarray(f(l0_norm_g).reshape(4, 8, 128).transpose(2, 0, 1).reshape(128, 32)),
        "gn1": np.ascontiguousarray(f(l1_norm_g).reshape(4, 8, 128).transpose(2, 0, 1).reshape(128, 32)),
        "w_in": f(l0_w_in), "w_qkv": f(l1_w_qkv),
        "w_out0": f(l0_w_out), "w_out1": f(l1_w_out),
        "w_gu0": f(l0_w_gu), "w_gu1": f(l1_w_gu),
        "w_down0": f(l0_w_down), "w_down1": f(l1_w_down),
        "convw": np.ascontiguousarray(f(l0_conv_w).reshape(31, 4, 128).transpose(2, 1, 0).reshape(128, 124)),
        "convb": _pp(l0_conv_b, 4), "lng": _pp(l0_conv_ln_g, 4), "lnb": _pp(l0_conv_ln_b, 4),
        "lam": np.ascontiguousarray(np.broadcast_to(f(l0_lambda).reshape(1, 256), (128, 256))),
        "subg": np.ascontiguousarray(f(l0_subln_g).reshape(128, 1)),
        "sink": np.ascontiguousarray(np.repeat(f(l1_sink).reshape(8, 2), 64, axis=1).T),
        "ident": np.eye(128, dtype=np.float32),
    }
    pidx = np.arange(128)
    partner = np.where((pidx % 32) < 16, pidx + 16, pidx - 16)
    perm = np.zeros((128, 128), np.float32)
    perm[partner, pidx] = 1.0
    shared["perm"] = perm
    for l, mb in ((0, l0_mod_b), (1, l1_mod_b)):
        shared["modb%d" % l] = np.ascontiguousarray(np.repeat(_pp(mb, 48), 2, axis=1))
    ropeAc, ropeAs = _rope_tables(np.arange(1024))
    shared["ropeAc"], shared["ropeAs"] = ropeAc, ropeAs

    in_maps = []
    for i in range(NCORES):
        b, qt = i // 4, i % 4
        base = 256 * qt - 128
        m = dict(shared)
        m["xp"] = np.ascontiguousarray(x_prompt[4 * i:4 * i + 4].reshape(1024, D).T)
        xs = x_sample[b]
        m["xsa"] = np.ascontiguousarray(xs.T)
        padded = np.zeros((1024 + 2 * 143, D), np.float32)
        padded[143:143 + 1024] = xs
        pe0 = 143 + base
        ext = np.concatenate([padded[pe0:pe0 + 512], padded[pe0 - 15:pe0], padded[pe0 + 512:pe0 + 527]], axis=0)
        m["xse"] = np.ascontiguousarray(ext.T)
        m["ck0T"] = np.ascontiguousarray(cache_k0[b].reshape(256, 512).T)
        m["cv0"] = np.ascontiguousarray(cache_v0[b].reshape(256, 512))
        k1t = cache_k1[b].reshape(256, 4, 64).transpose(1, 2, 0)
        m["ck1T"] = np.ascontiguousarray(np.concatenate([k1t, k1t], axis=1).reshape(512, 256))
        m["cv1"] = np.ascontiguousarray(cache_v1[b].reshape(256, 256))
        ct = np.stack([c_ctx.reshape(8, 128), c[b].reshape(8, 128)], axis=-1)
        m["condT"] = np.ascontiguousarray(ct.transpose(1, 0, 2).reshape(128, 16))
        epos = base + np.arange(512)
        m["ropeEc"], m["ropeEs"] = _rope_tables(epos)
        zpos = base - 15 + np.arange(542)
        vz = ((zpos >= 0) & (zpos < 1024)).astype(np.float32)
        m["valid"] = np.ascontiguousarray(np.broadcast_to(vz[None, :], (128, 542)))
        mk = np.zeros((128, 1024), np.float32)
        kk = np.arange(128)[:, None]
        ii = np.arange(128)[None, :]
        for k in range(4):
            kpos = base + 128 * k + kk
            for qi, q in enumerate((1, 2)):
                qpos = base + 128 * q + ii
                ok = (np.abs(qpos - kpos) <= 128) & (kpos >= 0) & (kpos < 1024)
                mk[:, 256 * k + 128 * qi:256 * k + 128 * (qi + 1)] = ok
        m["mask1"] = mk
        in_maps.append(m)

    return in_maps


def _assemble(R):
    y_prompt = np.empty((32, 256, D), np.float32)
    y_sample = np.empty((2, 1024, D), np.float32)
    new_k0 = np.empty((32, 256, 4, 2, 64), np.float32)
    new_v0 = np.empty((32, 256, 4, 128), np.float32)
    new_k1 = np.empty((32, 256, 4, 64), np.float32)
    new_v1 = np.empty((32, 256, 4, 64), np.float32)
    for i in range(NCORES):
        b, qt = i // 4, i % 4
        r = R[i]
        yT = np.asarray(r["yT"])
        y_prompt[4 * i:4 * i + 4] = yT[:, 0:1024].T.reshape(4, 256, D)
        y_sample[b, 256 * qt:256 * qt + 256] = yT[:, 1024:1280].T
        new_k0[4 * i:4 * i + 4] = np.asarray(r["k0T"]).T.reshape(4, 256, 4, 2, 64)
        new_v0[4 * i:4 * i + 4] = np.asarray(r["v0"]).reshape(4, 256, 4, 128)
        new_k1[4 * i:4 * i + 4] = np.asarray(r["k1T"]).T.reshape(4, 256, 4, 64)
        new_v1[4 * i:4 * i + 4] = np.asarray(r["v1"]).reshape(4, 256, 4, 64)
    return (y_prompt, y_sample, new_k0, new_v0, new_k1, new_v1)


def kernel(**inputs):
    if "nc" not in _NC_CACHE:
        _NC_CACHE["nc"] = build_program()
    nc = _NC_CACHE["nc"]
    in_maps = _prep(**inputs)
    res = run_bass_kernel_spmd(nc, in_maps, core_ids=list(range(NCORES)))
    return _assemble(res.results)
```
